# Optimizing a Trainium2 kernel written in Bass

```python
import math
import jax, jax.numpy as jnp
from jax import lax
import numpy as np

D_MODEL = 4096
BATCH = 2
SEQ = 8192
DEPTH = 2

D_FF = 11008
RMS_EPS = 1e-6
MLA_HEADS = 32
Q_LORA = 1024
KV_LORA = 512
QK_NOPE = 128
QK_ROPE = 64
QK_HEAD = QK_NOPE + QK_ROPE
V_HEAD = 128
ROPE_THETA = 10000.0
DIL_GROUPS = ((128, 1), (512, 4), (2048, 16))
N_GROUPS = 3
DIL_HEADS = 32
DIL_HEAD_DIM = 128
NUM_BUCKETS = 32
MAX_DISTANCE = 2048
Q_BLOCK = 128
N_A_LAYERS = DEPTH // 2
N_B_LAYERS = DEPTH - N_A_LAYERS

kernel_name = "yoco_mla_dilated_macaron_trunk"


def rms_norm(x, g):
    xf = x.astype(jnp.float32)
    y = xf * lax.rsqrt(jnp.mean(xf * xf, axis=-1, keepdims=True) + RMS_EPS)
    return (y * g.astype(jnp.float32)).astype(x.dtype)


def swiglu(x, wg, wu, wd):
    return (jax.nn.silu(x @ wg) * (x @ wu)) @ wd


def rope(x, pos):
    half = x.shape[-1] // 2
    inv = ROPE_THETA ** (-jnp.arange(half, dtype=jnp.float32) / half)
    ang = pos.astype(jnp.float32)[:, None] * inv[None, :]
    cos = jnp.cos(ang)[None, :, None, :]
    sin = jnp.sin(ang)[None, :, None, :]
    xf = x.astype(jnp.float32)
    x1, x2 = xf[..., :half], xf[..., half:]
    return jnp.concatenate([x1 * cos - x2 * sin, x2 * cos + x1 * sin], axis=-1).astype(x.dtype)


def causal_block_attention(q, k, v, scale):
    B, S, H, Dq = q.shape
    nq = S // Q_BLOCK
    qb = q.reshape(B, nq, Q_BLOCK, H, Dq).transpose(1, 0, 2, 3, 4)
    kpos = jnp.arange(S)

    def one_block(args):
        qblk, i = args
        s = jnp.einsum('bqhd,bkhd->bhqk', qblk, k, preferred_element_type=jnp.float32) * scale
        qpos = i * Q_BLOCK + jnp.arange(Q_BLOCK)
        s = jnp.where(kpos[None, :] <= qpos[:, None], s, -jnp.inf)
        p = jax.nn.softmax(s, axis=-1)
        return jnp.einsum('bhqk,bkhd->bqhd', p.astype(v.dtype), v)

    o = lax.map(one_block, (qb, jnp.arange(nq)))
    return o.transpose(1, 0, 2, 3, 4).reshape(B, S, H, v.shape[-1])


def mla_attention(xn, wdq, q_lora_norm, wuq, wdkv, kv_lora_norm, wukv, q_norm, k_norm, wo):
    B, S, _ = xn.shape
    pos = jnp.arange(S)
    cq = rms_norm(xn @ wdq, q_lora_norm)
    q = rms_norm((cq @ wuq).reshape(B, S, MLA_HEADS, QK_HEAD), q_norm)
    q = jnp.concatenate([q[..., :QK_NOPE], rope(q[..., QK_NOPE:], pos)], axis=-1)
    ckv = xn @ wdkv
    c_kv = rms_norm(ckv[..., :KV_LORA], kv_lora_norm)
    k_rope = ckv[..., KV_LORA:]
    kv = (c_kv @ wukv).reshape(B, S, MLA_HEADS, QK_NOPE + V_HEAD)
    k_nope, v = kv[..., :QK_NOPE], kv[..., QK_NOPE:]
    k = jnp.concatenate([k_nope, jnp.broadcast_to(k_rope[:, :, None, :], (B, S, MLA_HEADS, QK_ROPE))], axis=-1)
    k = rms_norm(k, k_norm)
    k = jnp.concatenate([k[..., :QK_NOPE], rope(k[..., QK_NOPE:], pos)], axis=-1)
    o = causal_block_attention(q, k, v, QK_HEAD ** -0.5)
    return o.reshape(B, S, MLA_HEADS * V_HEAD) @ wo


def t5_causal_bucket(dist):
    max_exact = NUM_BUCKETS // 2
    n = jnp.maximum(dist, 0)
    nf = jnp.maximum(n, 1).astype(jnp.float32)
    large = max_exact + (jnp.log(nf / max_exact) / math.log(MAX_DISTANCE / max_exact)
                         * (NUM_BUCKETS - max_exact)).astype(jnp.int32)
    large = jnp.minimum(large, NUM_BUCKETS - 1)
    return jnp.where(n < max_exact, n, large)


def dilated_group(q, k, v, bias_tab, window, dilation):
    B, S, H, Dh = q.shape
    span = dilation * Q_BLOCK
    Sp = -(-S // span) * span
    pad = Sp - S
    L = Sp // dilation
    nb = L // Q_BLOCK

    def to_blocks(t):
        t = jnp.pad(t, ((0, 0), (0, pad), (0, 0), (0, 0)))
        t = t.reshape(B, L, dilation, H, Dh).transpose(0, 2, 1, 3, 4)
        return t.reshape(B, dilation, nb, Q_BLOCK, H, Dh)

    def with_prev(t):
        prev = jnp.pad(t, ((0, 0), (0, 0), (1, 0), (0, 0), (0, 0), (0, 0)))[:, :, :-1]
        return jnp.concatenate([prev, t], axis=3)

    qb = to_blocks(q)
    kk = with_prev(to_blocks(k))
    vv = with_prev(to_blocks(v))
    r = jnp.arange(Q_BLOCK)[:, None]
    c = jnp.arange(2 * Q_BLOCK)[None, :]
    steps = Q_BLOCK + r - c
    band = (steps >= 0) & (steps <= window // dilation)
    first = (jnp.arange(nb)[:, None, None] > 0) | (c[None] >= Q_BLOCK)
    valid = band[None] & first
    bias = bias_tab[t5_causal_bucket(steps * dilation)]
    bias = bias.transpose(2, 0, 1).astype(jnp.float32)
    s = jnp.einsum('brnqhd,brnkhd->brnhqk', qb, kk,
                   preferred_element_type=jnp.float32) * (Dh ** -0.5) + bias
    s = jnp.where(valid[None, None, :, None], s, -jnp.inf)
    m = jnp.max(s, axis=-1, keepdims=True)
    p = jnp.exp(s - m)
    den = jnp.sum(p, axis=-1, keepdims=True)
    o = jnp.einsum('brnhqk,brnkhd->brnqhd', (p / den).astype(v.dtype), vv)
    lse = (m + jnp.log(den))[..., 0].transpose(0, 1, 2, 4, 3)

    def from_blocks(t):
        t = t.reshape((B, dilation, L) + t.shape[4:]).swapaxes(1, 2)
        return t.reshape((B, Sp) + t.shape[3:])[:, :S]

    return from_blocks(o), from_blocks(lse)


def shared_kv(h, kv_src_norm, w_kv_shared, k_norm_shared):
    B, S, _ = h.shape
    kv = (rms_norm(h, kv_src_norm) @ w_kv_shared).reshape(B, S, 2, N_GROUPS, DIL_HEADS, DIL_HEAD_DIM)
    k = rms_norm(kv[:, :, 0], k_norm_shared[:, None, :])
    v = kv[:, :, 1]
    return k, v


def dilated_attention(xn, wq, q_norm, k, v, rel_bias, wo):
    B, S, _ = xn.shape
    q = rms_norm((xn @ wq).reshape(B, S, N_GROUPS, DIL_HEADS, DIL_HEAD_DIM), q_norm[:, None, :])
    outs, lses = [], []
    for g, (window, dilation) in enumerate(DIL_GROUPS):
        o_g, l_g = dilated_group(q[:, :, g], k[:, :, g], v[:, :, g],
                                 rel_bias[:, g * DIL_HEADS:(g + 1) * DIL_HEADS], window, dilation)
        outs.append(o_g)
        lses.append(l_g)
    wts = jax.nn.softmax(jnp.stack(lses, axis=0), axis=0)
    o = jnp.einsum('gbsh,gbshd->bshd', wts.astype(xn.dtype), jnp.stack(outs, axis=0))
    return o.reshape(B, S, DIL_HEADS * DIL_HEAD_DIM) @ wo


def _w(k, shape, fan_in):
    return jax.random.normal(k, shape, jnp.float32) * (fan_in ** -0.5)


def _g(k, shape):
    return 1.0 + 0.01 * jax.random.normal(k, shape, jnp.float32)


def setup_inputs(seed: int = 0) -> dict:
    key = jax.random.key(seed)
    ks = jax.random.split(key, 24)
    NA, NB = N_A_LAYERS, N_B_LAYERS
    dil_w = N_GROUPS * DIL_HEADS * DIL_HEAD_DIM
    return {
        "x": jax.random.normal(ks[0], (BATCH, SEQ, D_MODEL), jnp.float32),
        "ffn_norm": _g(ks[1], (DEPTH, 2, D_MODEL)),
        "ffn_wg": _w(ks[2], (DEPTH, 2, D_MODEL, D_FF), D_MODEL),
        "ffn_wu": _w(ks[3], (DEPTH, 2, D_MODEL, D_FF), D_MODEL),
        "ffn_wd": _w(ks[4], (DEPTH, 2, D_FF, D_MODEL), D_FF),
        "attn_norm": _g(ks[5], (DEPTH, D_MODEL)),
        "mla_wdq": _w(ks[6], (NA, D_MODEL, Q_LORA), D_MODEL),
        "mla_q_lora_norm": _g(ks[7], (NA, Q_LORA)),
        "mla_wuq": _w(ks[8], (NA, Q_LORA, MLA_HEADS * QK_HEAD), Q_LORA),
        "mla_wdkv": _w(ks[9], (NA, D_MODEL, KV_LORA + QK_ROPE), D_MODEL),
        "mla_kv_lora_norm": _g(ks[10], (NA, KV_LORA)),
        "mla_wukv": _w(ks[11], (NA, KV_LORA, MLA_HEADS * (QK_NOPE + V_HEAD)), KV_LORA),
        "mla_q_norm": _g(ks[12], (NA, QK_HEAD)),
        "mla_k_norm": _g(ks[13], (NA, QK_HEAD)),
        "mla_wo": _w(ks[14], (NA, MLA_HEADS * V_HEAD, D_MODEL), MLA_HEADS * V_HEAD),
        "kv_src_norm": _g(ks[15], (D_MODEL,)),
        "w_kv_shared": _w(ks[16], (D_MODEL, 2 * dil_w), D_MODEL),
        "k_norm_shared": _g(ks[17], (N_GROUPS, DIL_HEAD_DIM)),
        "rel_bias": 0.5 * jax.random.normal(ks[18], (NUM_BUCKETS, N_GROUPS * DIL_HEADS), jnp.float32),
        "dil_wq": _w(ks[19], (NB, D_MODEL, dil_w), D_MODEL),
        "dil_q_norm": _g(ks[20], (NB, N_GROUPS, DIL_HEAD_DIM)),
        "dil_wo": _w(ks[21], (NB, DIL_HEADS * DIL_HEAD_DIM, D_MODEL), DIL_HEADS * DIL_HEAD_DIM),
    }


def reference(x, ffn_norm, ffn_wg, ffn_wu, ffn_wd, attn_norm, mla_wdq, mla_q_lora_norm, mla_wuq,
              mla_wdkv, mla_kv_lora_norm, mla_wukv, mla_q_norm, mla_k_norm, mla_wo,
              kv_src_norm, w_kv_shared, k_norm_shared, rel_bias, dil_wq, dil_q_norm, dil_wo):
    h = x
    k_sh, v_sh = None, None
    for l in range(DEPTH):
        if l == N_A_LAYERS:
            k_sh, v_sh = shared_kv(h, kv_src_norm, w_kv_shared, k_norm_shared)
        h = h + 0.5 * swiglu(rms_norm(h, ffn_norm[l, 0]), ffn_wg[l, 0], ffn_wu[l, 0], ffn_wd[l, 0])
        xn = rms_norm(h, attn_norm[l])
        if l < N_A_LAYERS:
            a = l
            h = h + mla_attention(xn, mla_wdq[a], mla_q_lora_norm[a], mla_wuq[a], mla_wdkv[a],
                                  mla_kv_lora_norm[a], mla_wukv[a], mla_q_norm[a], mla_k_norm[a], mla_wo[a])
        else:
            b = l - N_A_LAYERS
            h = h + dilated_attention(xn, dil_wq[b], dil_q_norm[b], k_sh, v_sh, rel_bias, dil_wo[b])
        h = h + 0.5 * swiglu(rms_norm(h, ffn_norm[l, 1]), ffn_wg[l, 1], ffn_wu[l, 1], ffn_wd[l, 1])
    return h
```

```python
import math
import os
import numpy as np
import concourse.bass as bass
import concourse.mybir as mybir
from concourse.bass_utils import run_bass_kernel_spmd

F32 = mybir.dt.float32
BF16 = mybir.dt.bfloat16
AF = mybir.ActivationFunctionType
ALU = mybir.AluOpType
RMS_EPS = 1e-6
SEM_ROT = 30000
DSEM_RETIRE = 20000

CFG_FULL = dict(D=4096, B=2, S=8192, FF=11008, H=32, QL=1024, KVL=512, DH=32,
                GROUPS=((128, 1), (512, 4), (2048, 16)), NBUCK=32, MAXD=2048, THETA=10000.0)


class Sem:
    def __init__(self, nc, name):
        self.h = nc.alloc_semaphore(name=name)
        self.v = 0


class Res:
    __slots__ = ("w", "r")

    def __init__(self):
        self.w = None
        self.r = {}


class Prog:
    ENG = ["sp", "act", "dve", "pool", "pe"]

    def __init__(self, nc):
        self.nc = nc
        self.q = {e: [] for e in self.ENG}
        self.nsem = 0
        self.pg = {e: self.new_sem("pg_" + e) for e in ["act", "dve", "pool", "pe"]}
        self.waited = {e: {} for e in self.ENG}
        self.nins = {e: 0 for e in self.ENG}
        self.pending = {e: [] for e in self.ENG}
        self.last_ev = {}
        self.dsems = []
        self.free_dsems = {'sp': [], 'pool': []}
        self.stage_dsems = []

    def new_sem(self, name):
        self.nsem += 1
        return Sem(self.nc, f"{name}_{self.nsem}")

    def dsem(self, kind="sp"):
        while self.free_dsems[kind] and self.free_dsems[kind][-1].v > DSEM_RETIRE:
            self.free_dsems[kind].pop()
        if self.free_dsems[kind]:
            s = self.free_dsems[kind].pop()
        else:
            s = self.new_sem("d" + kind)
            self.dsems.append(s)
        self.stage_dsems.append((kind, s))
        return s

    def release_stage_sems(self):
        for kind, s in self.stage_dsems:
            self.free_dsems[kind].append(s)
        self.stage_dsems = []

    def _wait(self, eng, sem, val):
        if eng == "pe" and sem is self.pg["pe"]:
            return
        w = self.waited[eng]
        if w.get(sem, 0) >= val:
            return
        w[sem] = val
        self.q[eng].append(lambda e, h=sem.h, v=val: e.wait_ge(h, v))
        self.nins[eng] += 1

    def _deps(self, eng, reads, writes):
        for b in reads:
            if b.w is not None:
                self._wait(eng, *b.w)
        for b in writes:
            if b.w is not None:
                self._wait(eng, *b.w)
            for sm, v in b.r.items():
                self._wait(eng, sm, v)

    def _commit(self, ev, reads, writes):
        sm, v = ev
        for b in reads:
            if b.r.get(sm, 0) < v:
                b.r[sm] = v
        for b in writes:
            b.w = ev
            b.r = {}

    def op(self, eng, fn, reads=(), writes=(), signal=True):
        self._deps(eng, reads, writes)
        self.nins[eng] += 1
        if not signal:
            self.q[eng].append(fn)
            self.pending[eng].append((reads, writes))
            return
        sem = self.pg[eng]
        if sem.v >= SEM_ROT:
            sem = self.pg[eng] = self.new_sem("pg_" + eng)
        sem.v += 1
        self.q[eng].append(lambda e, h=sem.h: fn(e).then_inc(h, 1))
        self.last_ev[eng] = (sem, sem.v)
        for r_, w_ in self.pending[eng]:
            self._commit((sem, sem.v), r_, w_)
        self.pending[eng] = []
        self._commit((sem, sem.v), reads, writes)

    def dma(self, eng, out, in_, sem, reads=(), writes=()):
        for b in writes:
            if b.w is not None and b.w[0] is sem and not b.r:
                b.w = None
        self._deps(eng, reads, writes)
        self.nins[eng] += 1
        sem.v += 16
        self.q[eng].append(lambda e, h=sem.h: e.dma_start(out=out, in_=in_).then_inc(h, 16))
        self._commit((sem, sem.v), reads, writes)

    def barrier(self):
        for e in self.ENG:
            assert not self.pending[e], e
        evs = list(self.last_ev.values())
        evs += [(s, s.v) for s in self.dsems if s.v > 0]
        for e in self.ENG:
            for sm, v in evs:
                self._wait(e, sm, v)

    def finish(self):
        self.barrier()
        nc = self.nc
        q = self.q
        with nc.Block() as block:
            @block.sync
            def _(e):
                for f in q["sp"]:
                    f(e)

            @block.scalar
            def _(e):
                for f in q["act"]:
                    f(e)

            @block.vector
            def _(e):
                for f in q["dve"]:
                    f(e)

            @block.gpsimd
            def _(e):
                for f in q["pool"]:
                    f(e)

            @block.tensor
            def _(e):
                for f in q["pe"]:
                    f(e)


SPLIT_TENSORS = True


class Arena:
    BASE = 17 * 1024
    TOP = 224 * 1024 - 2560

    def __init__(self, nc, words=None):
        self.nc = nc
        self.words = (self.TOP - self.BASE) // 4
        self.off = 0
        self.n = 0
        self.big = None if SPLIT_TENSORS else nc.alloc_sbuf_tensor("arena", [128, self.words], F32)

    def alloc(self, shape, dtype):
        n = int(np.prod(shape[1:]))
        w = (n + 1) // 2 if dtype == BF16 else n
        w = (w + 15) // 16 * 16
        assert self.off + w <= self.words, ("sbuf overflow", self.off, w)
        off = self.off
        self.off += w
        return self.view(off, shape, dtype)

    def view(self, off, shape, dtype):
        n = int(np.prod(shape[1:]))
        w = (n + 1) // 2 if dtype == BF16 else n
        assert off + w <= self.words
        if SPLIT_TENSORS:
            self.n += 1
            return self.nc.alloc_sbuf_tensor_at(f"sb{self.n}", [int(x) for x in shape], dtype, offset=self.BASE + off * 4)
        v = self.big[:, off:off + w]
        if dtype == BF16:
            v = v.bitcast(BF16)
        v = v[:, 0:n]
        if len(shape) == 3:
            v = v.rearrange("p (a b) -> p a b", b=shape[2])
        elif len(shape) == 4:
            v = v.rearrange("p (a b c) -> p a b c", b=shape[2], c=shape[3])
        if shape[0] < 128:
            v = v[0:shape[0]]
        return v


class Ring:
    REVIEW = 32

    def __init__(self, ctx, n, shape, dtype, dma=False):
        self.A = ctx.A
        self.shape, self.dtype = shape, dtype
        self.offs = []
        self.t = []
        for _ in range(n):
            self.offs.append(ctx.A.off)
            self.t.append(ctx.A.alloc(shape, dtype))
        self.r = [Res() for _ in range(n)]
        self.s = [ctx.P.dsem('pool' if dma == 'pool' else 'sp') for _ in range(n)] if dma else None
        self.i = 0
        self.n = n

    def next(self):
        k = self.i % self.n
        if SPLIT_TENSORS and self.i >= self.n and (self.i // self.n) % self.REVIEW == 0:
            self.t[k] = self.A.view(self.offs[k], self.shape, self.dtype)
        self.i += 1
        if self.s:
            return self.t[k], self.r[k], self.s[k]
        return self.t[k], self.r[k]


class Ctx:
    pass


def sl(ap, *idx):
    return ap[idx]


def rstd_from_ssq(ctx, ssq_ps, ssq_pr, out_t, out_r, n_feat, parts=128):
    P = ctx.P
    P.op("act", lambda e: e.activation(out=out_t, in_=ssq_ps, func=AF.Sqrt, scale=1.0 / n_feat, bias=ctx.eps_t[0:parts, 0:1]),
         reads=[ssq_pr, ctx.const_r], writes=[out_r])
    P.op("dve", lambda e: e.reciprocal(out=out_t, in_=out_t), reads=[out_r], writes=[out_r])


def norm_chunk(ctx, h_in, g_cols, c, xn, xn_r, xn_conf, hld, KT):
    P = ctx.P
    G = hld.t[0].shape[1]
    D = KT * 128
    ts = slice(c * 512, (c + 1) * 512)
    h_v = h_in.rearrange("(k p) t -> p k t", p=128)
    ssq_ps, ssq_pr = ctx.psum[0]
    for kg in range(KT // G):
        ht, hr, hs = hld.next()
        P.dma("sp", ht[:, :, :], h_v[:, kg * G:(kg + 1) * G, ts], hs, writes=[hr] + ctx.hld_conf.get(id(hr), []))
        for j in range(G):
            kt = kg * G + j
            sqt, sqr = ctx.sq.next()
            P.op("pool", lambda e, o=sqt, i=ht, j=j: e.tensor_tensor(out=o[:, :], in0=i[:, j, :], in1=i[:, j, :], op=ALU.mult),
                 reads=[hr], writes=[sqr])
            P.op("pe", lambda e, r=sqt, kt=kt: e.matmul(ssq_ps[:, :], lhsT=ctx.ones_f[:, :], rhs=r[:, :], start=(kt == 0), stop=(kt == KT - 1)),
                 reads=[sqr, ctx.const_r], writes=[ssq_pr])
    rstd_from_ssq(ctx, ssq_ps[:, :], ssq_pr, ctx.rstd[:, :], ctx.rstd_r, D)
    for kg in range(KT // G):
        ht, hr, hs = hld.next()
        P.dma("sp", ht[:, :, :], h_v[:, kg * G:(kg + 1) * G, ts], hs, writes=[hr] + ctx.hld_conf.get(id(hr), []))
        for j in range(G):
            kt = kg * G + j
            P.op("dve", lambda e, i=ht, j=j, kt=kt: e.scalar_tensor_tensor(out=xn[:, kt, :], in0=i[:, j, :], scalar=g_cols[:, kt:kt + 1], in1=ctx.rstd[:, :], op0=ALU.mult, op1=ALU.mult),
                 reads=[hr, ctx.rstd_r, ctx.const_r], writes=[xn_r[kt]] + xn_conf[kt])


def mm_acc(ctx, ps, ps_r, lhs_list, rhs_list, extra_reads):
    P = ctx.P
    n = len(lhs_list)
    for k in range(n):
        (la, lr), (ra, rr) = lhs_list[k], rhs_list[k]
        P.op("pe", lambda e, la=la, ra=ra, k=k: e.matmul(ps, lhsT=la, rhs=ra, start=(k == 0), stop=(k == n - 1)),
             reads=[lr, rr] + extra_reads, writes=[ps_r], signal=(k == n - 1))


def new_psum(ctx):
    if not SPLIT_TENSORS and getattr(ctx, "psum", None):
        return
    for g in reversed(getattr(ctx, "psum_guards", [])):
        g.__exit__(None, None, None)
    ctx.psum_guards = [ctx.nc.psum_tensor(f"ps{ctx.psum_gen}_{i}", [128, 512], F32) for i in range(8)]
    ctx.psum_gen += 1
    ctx.psum = [(g.__enter__(), Res()) for g in ctx.psum_guards]


def stage_begin(ctx):
    ctx.P.barrier()
    ctx.P.release_stage_sems()
    new_psum(ctx)
    ctx.A.off = ctx.arena_mark
    ctx.hld_conf = {}
    ctx.sq = Ring(ctx, 2, [128, 512], F32)
    ctx.rstd = ctx.A.alloc([128, 512], F32)
    ctx.rstd_r = Res()
    ctx.psi = 1


def stage_ffn(ctx, h_in, h_out, g_cols, wg_l, wu_l, wd_l, D, FF, S):
    P, A = ctx.P, ctx.A
    stage_begin(ctx)
    KT, FT, NCH = D // 128, FF // 128, S // 512
    G = min(4, KT)
    base = A.off
    xn = A.alloc([128, KT, 512], BF16)
    xn_r = [Res() for _ in range(KT)]
    hld = Ring(ctx, 2, [128, G, 512], F32, dma=True)
    WDW = FT * 64
    if A.off - base < 2 * WDW:
        A.off = base + 2 * WDW
    wd = [A.view(base + i * WDW, [128, FT, 128], BF16) for i in range(2)]
    wd_r = [Res() for _ in range(2)]
    wd_s = [P.dsem('pool') for _ in range(2)]

    def _ov(lo, hi):
        rs = [xn_r[kt] for kt in range(KT) if kt * 256 < hi and (kt + 1) * 256 > lo]
        for i in range(2):
            a = KT * 256 + i * G * 512
            if a < hi and a + G * 512 > lo:
                rs.append(hld.r[i])
        return rs
    wd_conf = [_ov(i * WDW, (i + 1) * WDW) for i in range(2)]
    xn_conf = [[wd_r[i] for i in range(2) if xn_r[kt] in wd_conf[i]] for kt in range(KT)]
    for j in range(2):
        ctx.hld_conf[id(hld.r[j])] = [wd_r[i] for i in range(2) if hld.r[j] in wd_conf[i]]
    act = A.alloc([128, FT, 512], BF16)
    act_r = [Res() for _ in range(FT)]
    wg = Ring(ctx, 2, [128, KT, 128], BF16, dma="pool")
    wu = Ring(ctx, 2, [128, KT, 128], BF16, dma="pool")
    sg = Ring(ctx, 2, [128, 512], F32)
    ot = Ring(ctx, 2, [128, 512], F32, dma=True)
    hr_ = Ring(ctx, 2, [128, 512], F32, dma=True)
    di = 0
    for c in range(NCH):
        ts = slice(c * 512, (c + 1) * 512)
        norm_chunk(ctx, h_in, g_cols, c, xn, xn_r, xn_conf, hld, KT)
        for f in range(FT):
            wgt, wgr, wgs = wg.next()
            wut, wur, wus = wu.next()
            P.dma("pool", wgt[:, :, :], wg_l[f].rearrange("p (k m) -> p k m", m=128), wgs, writes=[wgr])
            P.dma("pool", wut[:, :, :], wu_l[f].rearrange("p (k m) -> p k m", m=128), wus, writes=[wur])
            s = f % 2
            gp, gpr = ctx.psum[1 + 2 * s]
            up, upr = ctx.psum[2 + 2 * s]
            mm_acc(ctx, gp[:, :], gpr, [(wgt[:, kt, :], wgr) for kt in range(KT)], [(xn[:, kt, :], xn_r[kt]) for kt in range(KT)], [])
            mm_acc(ctx, up[:, :], upr, [(wut[:, kt, :], wur) for kt in range(KT)], [(xn[:, kt, :], xn_r[kt]) for kt in range(KT)], [])
            sgt, sgr = sg.next()
            P.op("act", lambda e, o=sgt, i=gp: e.activation(out=o[:, :], in_=i[:, :], func=AF.Silu), reads=[gpr], writes=[sgr])
            P.op("dve", lambda e, a=sgt, u=up, f=f: e.tensor_tensor(out=act[:, f, :], in0=u[:, :], in1=a[:, :], op=ALU.mult),
                 reads=[sgr, upr], writes=[act_r[f]])
        for n in range(KT):
            s = di % 2
            di += 1
            P.dma("pool", wd[s][:, :, :], wd_l[n].rearrange("p (k m) -> p k m", m=128), wd_s[s], writes=[wd_r[s]] + wd_conf[s])
            hrt, hrr, hrs = hr_.next()
            P.dma("sp", hrt[:, :], h_in[n * 128:(n + 1) * 128, ts], hrs, writes=[hrr])
            op_, opr = ctx.psum[5 + s]
            mm_acc(ctx, op_[:, :], opr, [(wd[s][:, ft, :], wd_r[s]) for ft in range(FT)], [(act[:, ft, :], act_r[ft]) for ft in range(FT)], [])
            ott, otr, ots = ot.next()
            P.op("dve", lambda e, o=ott, i=op_, h=hrt: e.scalar_tensor_tensor(out=o[:, :], in0=i[:, :], scalar=0.5, in1=h[:, :], op0=ALU.mult, op1=ALU.add),
                 reads=[opr, hrr], writes=[otr])
            P.dma("sp", h_out[n * 128:(n + 1) * 128, ts], ott[:, :], ots, reads=[otr])


def stage_oproj(ctx, oT, w_l, h_in, h_out, D, KI, S):
    P, A = ctx.P, ctx.A
    stage_begin(ctx)
    KT, NCH = D // 128, S // 512
    ob = Ring(ctx, 2, [128, KI, 512], BF16, dma=True)
    w = Ring(ctx, 2, [128, KI, 128], BF16, dma="pool")
    ot = Ring(ctx, 2, [128, 512], F32, dma=True)
    hr_ = Ring(ctx, 2, [128, 512], F32, dma=True)
    o_v = oT.rearrange("(k p) t -> p k t", p=128)
    for c in range(NCH):
        ts = slice(c * 512, (c + 1) * 512)
        obt, obr, obs = ob.next()
        P.dma("sp", obt[:, :, :], o_v[:, :, ts], obs, writes=[obr])
        for n in range(KT):
            wt, wr, ws = w.next()
            P.dma("pool", wt[:, :, :], w_l[n].rearrange("p (k m) -> p k m", m=128), ws, writes=[wr])
            hrt, hrr, hrs = hr_.next()
            P.dma("sp", hrt[:, :], h_in[n * 128:(n + 1) * 128, ts], hrs, writes=[hrr])
            op_, opr = ctx.psum[1 + n % 2]
            mm_acc(ctx, op_[:, :], opr, [(wt[:, k, :], wr) for k in range(KI)], [(obt[:, k, :], obr) for k in range(KI)], [])
            ott, otr, ots = ot.next()
            P.op("dve", lambda e, o=ott, i=op_, h=hrt: e.tensor_tensor(out=o[:, :], in0=i[:, :], in1=h[:, :], op=ALU.add),
                 reads=[opr, hrr], writes=[otr])
            P.dma("sp", h_out[n * 128:(n + 1) * 128, ts], ott[:, :], ots, reads=[otr])


def headnorm_epilogue(ctx, ps, psr, parts, g_col, n_feat, out_dram, ssq_extra=None, scale=None, tmp=None):
    P = ctx.P
    xs, xsr = ctx.xs.next()
    P.op("act", lambda e: e.activation(out=xs[0:parts, :], in_=ps, func=AF.Copy), reads=[psr], writes=[xsr])
    sqt, sqr = ctx.sq.next()
    P.op("pool", lambda e: e.tensor_tensor(out=sqt[0:parts, :], in0=xs[0:parts, :], in1=xs[0:parts, :], op=ALU.mult), reads=[xsr], writes=[sqr])
    ssq_ps, ssq_pr = ctx.psum[0]
    P.op("pe", lambda e: e.matmul(ssq_ps[:, :], lhsT=ctx.ones_f[0:parts, :], rhs=sqt[0:parts, :], start=True, stop=True),
         reads=[sqr, ctx.const_r], writes=[ssq_pr])
    rt, rr = ctx.rt.next()
    rstd_from_ssq(ctx, ssq_ps[0:parts, :], ssq_pr, rt[0:parts, :], rr, n_feat, parts)
    ob, obr, obs = ctx.outb.next()
    P.op("dve", lambda e: e.scalar_tensor_tensor(out=ob[0:parts, :], in0=xs[0:parts, :], scalar=g_col, in1=rt[0:parts, :], op0=ALU.mult, op1=ALU.mult),
         reads=[xsr, rr, ctx.const_r], writes=[obr])
    P.dma("sp", out_dram, ob[0:parts, :], obs, reads=[obr])


def stage_proj_headnorm(ctx, h_in, g_cols, w_l, NT, gn_sb, gmap, out_T, D, S, v_w=None, v_out=None, NV=0):
    P, A = ctx.P, ctx.A
    stage_begin(ctx)
    KT, NCH = D // 128, S // 512
    G = min(4, KT)
    xn = A.alloc([128, KT, 512], BF16)
    xn_r = [Res() for _ in range(KT)]
    xn_conf = [[] for _ in range(KT)]
    hld = Ring(ctx, 2, [128, G, 512], F32, dma=True)
    w = Ring(ctx, 2, [128, KT, 128], BF16, dma="pool")
    ctx.xs = Ring(ctx, 2, [128, 512], F32)
    ctx.rt = Ring(ctx, 2, [128, 512], F32)
    ctx.outb = Ring(ctx, 2, [128, 512], BF16, dma=True)
    if v_w is not None:
        wv = Ring(ctx, 2, [128, KT, 512], BF16, dma="pool")
        vb = Ring(ctx, 2, [128, 512], BF16, dma=True)
        v_wv = v_w.rearrange("(k p) n -> p k n", p=128)
    for c in range(NCH):
        ts = slice(c * 512, (c + 1) * 512)
        norm_chunk(ctx, h_in, g_cols, c, xn, xn_r, xn_conf, hld, KT)
        for n in range(NT):
            wt, wr, ws = w.next()
            P.dma("pool", wt[:, :, :], w_l[n].rearrange("p (k m) -> p k m", m=128), ws, writes=[wr])
            ps, psr = ctx.psum[1 + n % 2]
            mm_acc(ctx, ps[:, :], psr, [(wt[:, k, :], wr) for k in range(KT)], [(xn[:, k, :], xn_r[k]) for k in range(KT)], [])
            headnorm_epilogue(ctx, ps[:, :], psr, 128, gn_sb[:, gmap(n):gmap(n) + 1], 128, out_T[n, :, ts])
        if v_w is not None:
            for nv in range(NV):
                wt, wr, ws = wv.next()
                q4 = max(1, KT // 4)
                for a in range(0, KT, q4):
                    P.dma("pool", wt[:, a:a + q4, :], v_wv[:, a:a + q4, nv * 512:(nv + 1) * 512], ws, writes=[wr])
                for tb in range(4):
                    ps, psr = ctx.psum[3 + tb]
                    mm_acc(ctx, ps[:, :], psr, [(xn[:, k, tb * 128:(tb + 1) * 128], xn_r[k]) for k in range(KT)], [(wt[:, k, :], wr) for k in range(KT)], [])
                    vt, vr, vs = vb.next()
                    P.op("act", lambda e, o=vt, i=ps: e.activation(out=o[:, :], in_=i[:, :], func=AF.Copy), reads=[psr], writes=[vr])
                    P.dma("sp", v_out[c * 512 + tb * 128:c * 512 + (tb + 1) * 128, nv * 512:(nv + 1) * 512], vt[:, :], vs, reads=[vr])


def rope_apply(ctx, xr, xr_r, cs, cs_r, sn, sn_r, out_dram):
    P = ctx.P
    rp, rpr = ctx.psum[7]
    P.op("pe", lambda e: e.matmul(rp[0:64, :], lhsT=ctx.rm[0:64, :], rhs=xr[0:64, :], start=True, stop=True),
         reads=[xr_r, ctx.const_r], writes=[rpr])
    t1, t1r = ctx.xs.next()
    P.op("dve", lambda e: e.tensor_tensor(out=t1[0:64, :], in0=rp[0:64, :], in1=sn[0:64, :], op=ALU.mult), reads=[rpr, sn_r], writes=[t1r])
    t2, t2r = ctx.sq.next()
    P.op("pool", lambda e: e.tensor_tensor(out=t2[0:64, :], in0=xr[0:64, :], in1=cs[0:64, :], op=ALU.mult), reads=[xr_r, cs_r], writes=[t2r])
    ob, obr, obs = ctx.outb.next()
    P.op("dve", lambda e: e.tensor_tensor(out=ob[0:64, :], in0=t1[0:64, :], in1=t2[0:64, :], op=ALU.add), reads=[t1r, t2r], writes=[obr])
    P.dma("sp", out_dram, ob[0:64, :], obs, reads=[obr])


def stage_mla_proj(ctx, h_in, g_cols, T, cfg):
    P, A = ctx.P, ctx.A
    stage_begin(ctx)
    D, S, H, QL, KVL = cfg["D"], cfg["S"], cfg["H"], cfg["QL"], cfg["KVL"]
    KT, NCH, QT_, KVT = D // 128, S // 512, QL // 128, KVL // 128
    G = min(4, KT)
    xn = A.alloc([128, KT, 512], BF16)
    xn_r = [Res() for _ in range(KT)]
    xn_conf = [[] for _ in range(KT)]
    hld = Ring(ctx, 2, [128, G, 512], F32, dma=True)
    w = Ring(ctx, 2, [128, KT, 128], BF16, dma="pool")
    wq = Ring(ctx, 2, [128, QT_, 192], BF16, dma="pool")
    cq = A.alloc([128, QT_, 512], F32)
    cq_r = [Res() for _ in range(QT_)]
    cqn = A.alloc([128, QT_, 512], BF16)
    cqn_r = [Res() for _ in range(QT_)]
    ckv = A.alloc([128, KVT, 512], F32)
    ckv_r = [Res() for _ in range(KVT)]
    ctx.xs = Ring(ctx, 3, [128, 512], F32)
    ctx.rt = Ring(ctx, 2, [128, 512], F32)
    ctx.outb = Ring(ctx, 3, [128, 512], BF16, dma=True)
    cs = Ring(ctx, 2, [64, 512], F32, dma=True)
    sn = Ring(ctx, 2, [64, 512], F32, dma=True)
    xr = Ring(ctx, 4, [64, 512], F32)
    rq = A.alloc([128, 512], F32)
    rq_r = Res()
    srow = Ring(ctx, 2, [1, 512], F32, dma=True)
    xn_rhs = [(xn[:, k, :], xn_r[k]) for k in range(KT)]
    for c in range(NCH):
        ts = slice(c * 512, (c + 1) * 512)
        norm_chunk(ctx, h_in, g_cols, c, xn, xn_r, xn_conf, hld, KT)
        cst, csr, css = cs.next()
        snt, snr, sns = sn.next()
        P.dma("sp", cst[:, :], T["cos2"][:, ts], css, writes=[csr])
        P.dma("sp", snt[:, :], T["sinS"][:, ts], sns, writes=[snr])
        ssq2, ssq2r = ctx.psum[6]
        for n in range(QT_):
            wt, wr, ws = w.next()
            P.dma("pool", wt[:, :, :], T["wdq_l"][n].rearrange("p (k m) -> p k m", m=128), ws, writes=[wr])
            ps, psr = ctx.psum[1 + n % 2]
            mm_acc(ctx, ps[:, :], psr, [(wt[:, k, :], wr) for k in range(KT)], xn_rhs, [])
            P.op("act", lambda e, n=n, ps=ps: e.activation(out=cq[:, n, :], in_=ps[:, :], func=AF.Copy), reads=[psr], writes=[cq_r[n]])
            sqt, sqr = ctx.sq.next()
            P.op("pool", lambda e, n=n, o=sqt: e.tensor_tensor(out=o[:, :], in0=cq[:, n, :], in1=cq[:, n, :], op=ALU.mult), reads=[cq_r[n]], writes=[sqr])
            P.op("pe", lambda e, r=sqt, n=n: e.matmul(ssq2[:, :], lhsT=ctx.ones_f[:, :], rhs=r[:, :], start=(n == 0), stop=(n == QT_ - 1)),
                 reads=[sqr, ctx.const_r], writes=[ssq2r])
        rstd_from_ssq(ctx, ssq2[:, :], ssq2r, rq[:, :], rq_r, QL)
        for n in range(QT_):
            P.op("dve", lambda e, n=n: e.scalar_tensor_tensor(out=cqn[:, n, :], in0=cq[:, n, :], scalar=T["qlora_sb"][:, n:n + 1], in1=rq[:, :], op0=ALU.mult, op1=ALU.mult),
                 reads=[cq_r[n], rq_r, ctx.const_r], writes=[cqn_r[n]])
        for n in range(KVT + 1):
            wt, wr, ws = w.next()
            P.dma("pool", wt[:, :, :], T["wdkv_l"][n].rearrange("p (k m) -> p k m", m=128), ws, writes=[wr])
            ps, psr = ctx.psum[1 + n % 2]
            if n < KVT:
                mm_acc(ctx, ps[:, :], psr, [(wt[:, k, :], wr) for k in range(KT)], xn_rhs, [])
                P.op("act", lambda e, n=n, ps=ps: e.activation(out=ckv[:, n, :], in_=ps[:, :], func=AF.Copy), reads=[psr], writes=[ckv_r[n]])
                sqt, sqr = ctx.sq.next()
                P.op("pool", lambda e, n=n, o=sqt: e.tensor_tensor(out=o[:, :], in0=ckv[:, n, :], in1=ckv[:, n, :], op=ALU.mult), reads=[ckv_r[n]], writes=[sqr])
                P.op("pe", lambda e, r=sqt, n=n: e.matmul(ssq2[:, :], lhsT=ctx.ones_f[:, :], rhs=r[:, :], start=(n == 0), stop=(n == KVT - 1)),
                     reads=[sqr, ctx.const_r], writes=[ssq2r])
            else:
                mm_acc(ctx, ps[0:64, :], psr, [(wt[:, k, 0:64], wr) for k in range(KT)], xn_rhs, [])
                xrt, xrr = xr.next()
                P.op("act", lambda e, o=xrt, ps=ps: e.activation(out=o[0:64, :], in_=ps[0:64, :], func=AF.Copy), reads=[psr], writes=[xrr])
                sqt, sqr = ctx.sq.next()
                P.op("pool", lambda e, o=sqt, i=xrt: e.tensor_tensor(out=o[0:64, :], in0=i[0:64, :], in1=i[0:64, :], op=ALU.mult), reads=[xrr], writes=[sqr])
                rp, rpr = ctx.psum[7]
                P.op("pe", lambda e, r=sqt: e.matmul(rp[0:1, :], lhsT=ctx.ones_f[0:64, 0:1], rhs=r[0:64, :], start=True, stop=True),
                     reads=[sqr, ctx.const_r], writes=[rpr])
                srt, srr, srs = srow.next()
                P.op("act", lambda e, o=srt: e.activation(out=o[0:1, :], in_=rp[0:1, :], func=AF.Copy), reads=[rpr], writes=[srr])
                P.dma("sp", T["ssqr"][0:1, ts], srt[0:1, :], srs, reads=[srr])
                xg, xgr = xr.next()
                P.op("dve", lambda e, o=xg, i=xrt: e.tensor_scalar(out=o[0:64, :], in0=i[0:64, :], scalar1=T["gk_sb"][0:64, 1:2], scalar2=None, op0=ALU.mult),
                     reads=[xrr, ctx.const_r], writes=[xgr])
                rope_apply(ctx, xg, xgr, cst, csr, snt, snr, T["krT"][:, ts])
        rstd_from_ssq(ctx, ssq2[:, :], ssq2r, rq[:, :], rq_r, KVL)
        for n in range(KVT):
            ob, obr, obs = ctx.outb.next()
            P.op("dve", lambda e, n=n, ob=ob: e.scalar_tensor_tensor(out=ob[:, :], in0=ckv[:, n, :], scalar=T["kvlora_sb"][:, n:n + 1], in1=rq[:, :], op0=ALU.mult, op1=ALU.mult),
                 reads=[ckv_r[n], rq_r, ctx.const_r], writes=[obr])
            P.dma("sp", T["ckvT"][n * 128:(n + 1) * 128, ts], ob[:, :], obs, reads=[obr])
        cq_rhs = [(cqn[:, k, :], cqn_r[k]) for k in range(QT_)]
        for h in range(H):
            wt, wr, ws = wq.next()
            P.dma("pool", wt[:, :, :], T["wuq_l"][h].rearrange("p (k m) -> p k m", m=192), ws, writes=[wr])
            pn, pnr = ctx.psum[1 + h % 2]
            pr_, prr = ctx.psum[3 + h % 2]
            mm_acc(ctx, pn[:, :], pnr, [(wt[:, k, 0:128], wr) for k in range(QT_)], cq_rhs, [])
            mm_acc(ctx, pr_[0:64, :], prr, [(wt[:, k, 128:192], wr) for k in range(QT_)], cq_rhs, [])
            xs, xsr = ctx.xs.next()
            P.op("act", lambda e, xs=xs, pn=pn: e.activation(out=xs[:, :], in_=pn[:, :], func=AF.Copy), reads=[pnr], writes=[xsr])
            xrt, xrr = xr.next()
            P.op("act", lambda e, o=xrt, pr_=pr_: e.activation(out=o[0:64, :], in_=pr_[0:64, :], func=AF.Copy), reads=[prr], writes=[xrr])
            sq1, sq1r = ctx.sq.next()
            P.op("pool", lambda e, o=sq1, xs=xs: e.tensor_tensor(out=o[:, :], in0=xs[:, :], in1=xs[:, :], op=ALU.mult), reads=[xsr], writes=[sq1r])
            sq2, sq2r = ctx.sq.next()
            P.op("pool", lambda e, o=sq2, i=xrt: e.tensor_tensor(out=o[0:64, :], in0=i[0:64, :], in1=i[0:64, :], op=ALU.mult), reads=[xrr], writes=[sq2r])
            ssq_ps, ssq_pr = ctx.psum[0]
            P.op("pe", lambda e, r=sq1: e.matmul(ssq_ps[:, :], lhsT=ctx.ones_f[:, :], rhs=r[:, :], start=True, stop=False),
                 reads=[sq1r, ctx.const_r], writes=[ssq_pr], signal=False)
            P.op("pe", lambda e, r=sq2: e.matmul(ssq_ps[:, :], lhsT=ctx.ones_f[0:64, :], rhs=r[0:64, :], start=False, stop=True),
                 reads=[sq2r, ctx.const_r], writes=[ssq_pr])
            rt, rr = ctx.rt.next()
            rstd_from_ssq(ctx, ssq_ps[:, :], ssq_pr, rt[:, :], rr, 192)
            ob, obr, obs = ctx.outb.next()
            P.op("dve", lambda e, ob=ob, xs=xs, rt=rt: e.scalar_tensor_tensor(out=ob[:, :], in0=xs[:, :], scalar=T["gq_sb"][:, 0:1], in1=rt[:, :], op0=ALU.mult, op1=ALU.mult),
                 reads=[xsr, rr, ctx.const_r], writes=[obr])
            P.dma("sp", T["qT"][h, 0:128, ts], ob[:, :], obs, reads=[obr])
            xg, xgr = xr.next()
            P.op("dve", lambda e, o=xg, i=xrt, rt=rt: e.scalar_tensor_tensor(out=o[0:64, :], in0=i[0:64, :], scalar=T["gq_sb"][0:64, 1:2], in1=rt[0:64, :], op0=ALU.mult, op1=ALU.mult),
                 reads=[xrr, rr, ctx.const_r], writes=[xgr])
            rope_apply(ctx, xg, xgr, cst, csr, snt, snr, T["qT"][h, 128:192, ts])


def stage_mla_attn(ctx, T, cfg):
    P, A = ctx.P, ctx.A
    stage_begin(ctx)
    S, H, KVL = cfg["S"], cfg["H"], cfg["KVL"]
    NCH, KVT, NB = S // 512, KVL // 128, S // 128
    scale = 192.0 ** -0.5
    Kn = A.alloc([128, S], BF16); Kn_r = [Res() for _ in range(NCH)]
    Kr = A.alloc([64, S], BF16); Kr_r = [Res() for _ in range(NCH)]
    V = A.alloc([128, NB, 128], BF16); V_r = [Res() for _ in range(NCH)]
    Qn = A.alloc([128, S], BF16); Qn_r = Res(); Qn_s = P.dsem()
    Qr = A.alloc([64, S], BF16); Qr_r = Res(); Qr_s = P.dsem()
    msk = A.alloc([128, 4, 512], BF16); msk_r = Res(); msk_s = P.dsem('pool')
    P.dma("pool", msk[:, :, :], T["cmask"].rearrange("m p q -> p m q"), msk_s, writes=[msk_r])
    ck = Ring(ctx, 2, [128, KVT, 512], BF16, dma=True)
    krc = Ring(ctx, 2, [64, 512], BF16, dma=True)
    srow = Ring(ctx, 2, [1, 512], F32, dma=True)
    wkv = Ring(ctx, 2, [128, KVT, 256], BF16, dma="pool")
    ctx.xs = Ring(ctx, 2, [128, 512], F32)
    ctx.rt = Ring(ctx, 2, [128, 512], F32)
    pT = Ring(ctx, 3, [128, 512], BF16)
    rden = Ring(ctx, 2, [128, 512], F32)
    ctx.outb = Ring(ctx, 2, [128, 512], BF16, dma=True)
    ckv_v = T["ckvT"].rearrange("(k p) t -> p k t", p=128)
    for h in range(H):
        wt, wr, ws = wkv.next()
        P.dma("pool", wt[:, :, :], T["wukv_l"][h].rearrange("p (k m) -> p k m", m=256), ws, writes=[wr])
        P.dma("sp", Qn[:, :], T["qT"][h, 0:128, :], Qn_s, writes=[Qn_r])
        P.dma("sp", Qr[:, :], T["qT"][h, 128:192, :], Qr_s, writes=[Qr_r])
        for c in range(NCH):
            ts = slice(c * 512, (c + 1) * 512)
            ct, cr, cs_ = ck.next()
            P.dma("sp", ct[:, :, :], ckv_v[:, :, ts], cs_, writes=[cr])
            kt_, krr, krs = krc.next()
            P.dma("sp", kt_[:, :], T["krT"][:, ts], krs, writes=[krr])
            srt, srr, srs = srow.next()
            P.dma("sp", srt[0:1, :], T["ssqr"][0:1, ts], srs, writes=[srr])
            kp, kpr = ctx.psum[1 + c % 2]
            mm_acc(ctx, kp[:, :], kpr, [(wt[:, k, 0:128], wr) for k in range(KVT)], [(ct[:, k, :], cr) for k in range(KVT)], [])
            xs, xsr = ctx.xs.next()
            P.op("act", lambda e, xs=xs, kp=kp: e.activation(out=xs[:, :], in_=kp[:, :], func=AF.Copy), reads=[kpr], writes=[xsr])
            sqt, sqr = ctx.sq.next()
            P.op("pool", lambda e, o=sqt, xs=xs: e.tensor_tensor(out=o[:, :], in0=xs[:, :], in1=xs[:, :], op=ALU.mult), reads=[xsr], writes=[sqr])
            ssq_ps, ssq_pr = ctx.psum[0]
            P.op("pe", lambda e, r=sqt: e.matmul(ssq_ps[:, :], lhsT=ctx.ones_f[:, :], rhs=r[:, :], start=True, stop=False),
                 reads=[sqr, ctx.const_r], writes=[ssq_pr], signal=False)
            P.op("pe", lambda e, r=srt: e.matmul(ssq_ps[:, :], lhsT=ctx.ones_f[0:1, :], rhs=r[0:1, :], start=False, stop=True),
                 reads=[srr, ctx.const_r], writes=[ssq_pr])
            rt, rr = ctx.rt.next()
            rstd_from_ssq(ctx, ssq_ps[:, :], ssq_pr, rt[:, :], rr, 192)
            P.op("dve", lambda e, xs=xs, rt=rt, ts=ts: e.scalar_tensor_tensor(out=Kn[:, ts], in0=xs[:, :], scalar=T["gk_sb"][:, 0:1], in1=rt[:, :], op0=ALU.mult, op1=ALU.mult),
                 reads=[xsr, rr, ctx.const_r], writes=[Kn_r[c]])
            P.op("dve", lambda e, i=kt_, rt=rt, ts=ts: e.tensor_tensor(out=Kr[0:64, ts], in0=i[0:64, :], in1=rt[0:64, :], op=ALU.mult),
                 reads=[krr, rr], writes=[Kr_r[c]])
            vp, vpr = ctx.psum[3 + c % 2]
            for tb in range(4):
                mm_acc(ctx, vp[:, tb * 128:(tb + 1) * 128], vpr, [(ct[:, k, tb * 128:(tb + 1) * 128], cr) for k in range(KVT)], [(wt[:, k, 128:256], wr) for k in range(KVT)], [])
            P.op("act", lambda e, c=c, vp=vp: e.activation(out=V[:, c * 4:(c + 1) * 4, :], in_=vp[:, :].rearrange("p (a b) -> p a b", b=128), func=AF.Copy),
                 reads=[vpr], writes=[V_r[c]])
        for J in range(NCH):
            qs = slice(J * 512, (J + 1) * 512)
            op_, opr = ctx.psum[5] if J % 2 == 0 else ctx.psum[7]
            dp, dpr = ctx.psum[6] if J % 2 == 0 else ctx.psum[4]
            nkb = 4 * J + 4
            for kb in range(nkb):
                kc = kb // 4
                ks = slice(kb * 128, (kb + 1) * 128)
                sp_, spr = ctx.psum[1 + kb % 2]
                P.op("pe", lambda e, sp_=sp_, ks=ks, qs=qs: e.matmul(sp_[:, :], lhsT=Kn[:, ks], rhs=Qn[:, qs], start=True, stop=False),
                     reads=[Kn_r[kc], Qn_r], writes=[spr], signal=False)
                P.op("pe", lambda e, sp_=sp_, ks=ks, qs=qs: e.matmul(sp_[:, :], lhsT=Kr[0:64, ks], rhs=Qr[0:64, qs], start=False, stop=True),
                     reads=[Kr_r[kc], Qr_r], writes=[spr])
                pt, ptr = pT.next()
                P.op("act", lambda e, pt=pt, sp_=sp_: e.activation(out=pt[:, :], in_=sp_[:, :], func=AF.Exp, scale=scale), reads=[spr], writes=[ptr])
                m = kb - 4 * J
                if m >= 0:
                    P.op("dve", lambda e, pt=pt, m=m: e.tensor_tensor(out=pt[:, :], in0=pt[:, :], in1=msk[:, m, :], op=ALU.mult), reads=[ptr, msk_r], writes=[ptr])
                P.op("pe", lambda e, pt=pt, kb=kb, op_=op_, nkb=nkb: e.matmul(op_[:, :], lhsT=V[:, kb, :], rhs=pt[:, :], start=(kb == 0), stop=(kb == nkb - 1)),
                     reads=[V_r[kc], ptr], writes=[opr], signal=False)
                P.op("pe", lambda e, pt=pt, kb=kb, dp=dp, nkb=nkb: e.matmul(dp[:, :], lhsT=ctx.ones_b[:, :], rhs=pt[:, :], start=(kb == 0), stop=(kb == nkb - 1)),
                     reads=[ptr, ctx.const_r], writes=[dpr, opr])
            rd, rdr = rden.next()
            P.op("dve", lambda e, rd=rd, dp=dp: e.reciprocal(out=rd[:, :], in_=dp[:, :]), reads=[dpr], writes=[rdr])
            ob, obr, obs = ctx.outb.next()
            P.op("dve", lambda e, ob=ob, rd=rd, op_=op_: e.tensor_tensor(out=ob[:, :], in0=op_[:, :], in1=rd[:, :], op=ALU.mult), reads=[opr, rdr], writes=[obr])
            P.dma("sp", T["oT"][h * 128:(h + 1) * 128, qs], ob[:, :], obs, reads=[obr])


def stage_bias_table(ctx, T, cfg):
    P, A = ctx.P, ctx.A
    stage_begin(ctx)
    DH, NBK = cfg["DH"], cfg["NBUCK"]
    rb = A.alloc([NBK, 3 * DH], F32); rb_r = Res(); rb_s = P.dsem()
    P.dma("sp", rb[:, :], T["rel_bias"], rb_s, writes=[rb_r])
    oh = A.alloc([NBK, 3, 384], F32); oh_r = Res(); oh_s = P.dsem()
    P.dma("sp", oh[:, :, :], T["onehot"].rearrange("g b z -> b g z"), oh_s, writes=[oh_r])
    zm = A.alloc([DH, 384], F32); zm_r = Res(); zm_s = P.dsem()
    P.dma("sp", zm[:, :], T["zmask"], zm_s, writes=[zm_r])
    ed = Ring(ctx, 2, [DH, 384], F32, dma=True)
    for g in range(3):
        ps, psr = ctx.psum[1 + g % 2]
        P.op("pe", lambda e, g=g, ps=ps: e.matmul(ps[0:DH, 0:384], lhsT=rb[:, g * DH:(g + 1) * DH], rhs=oh[:, g, :], start=True, stop=True),
             reads=[rb_r, oh_r], writes=[psr])
        et, er, es = ed.next()
        P.op("act", lambda e, et=et, ps=ps: e.activation(out=et[:, :], in_=ps[0:DH, 0:384], func=AF.Exp), reads=[psr], writes=[er])
        P.op("dve", lambda e, et=et: e.tensor_tensor(out=et[:, :], in0=et[:, :], in1=zm[:, :], op=ALU.mult), reads=[er, zm_r], writes=[er])
        P.dma("sp", T["Ed"][g * DH:(g + 1) * DH, :], et[:, :], es, reads=[er])


def stage_dil_attn(ctx, T, cfg):
    P, A = ctx.P, ctx.A
    stage_begin(ctx)
    S, DH, GROUPS = cfg["S"], cfg["DH"], cfg["GROUPS"]
    NBLK = S // 128
    VW = 3 * DH * 128
    acc_o = A.alloc([128, S], F32); acc_o_r = Res()
    acc_d = A.alloc([128, S], F32); acc_d_r = Res()
    Qb = Ring(ctx, 2, [128, S], BF16, dma=True)
    Kb = Ring(ctx, 2, [128, S], BF16, dma=True)
    Vb = Ring(ctx, 2, [128, NBLK, 128], BF16, dma=True)
    Eb = Ring(ctx, 2, [128, 2, 128], F32)
    Hb = Ring(ctx, 2, [128, 2, 128], F32, dma=True)
    eb = Ring(ctx, 2, [128, 512], F32)
    pT = Ring(ctx, 2, [128, 512], BF16)
    ctx.outb = Ring(ctx, 2, [128, 512], BF16, dma=True)
    ed_t = T["Ed"].tensor
    for h in range(DH):
        for g, (win, d) in enumerate(GROUPS):
            gh = g * DH + h
            nb = NBLK // d
            qt, qr, qs_ = Qb.next()
            kt, kr_, ks_ = Kb.next()
            vt, vr, vs = Vb.next()
            P.dma("sp", qt[:, :], T["qdT"][gh, :, :], qs_, writes=[qr])
            P.dma("sp", kt[:, :], T["kshT"][gh, :, :], ks_, writes=[kr_])
            for r in range(d):
                for nl in range(0, nb, 16):
                    cnt = min(16, nb - nl)
                    src = bass.AP(T["vsh"].tensor, (r + d * 128 * nl) * VW + gh * 128, [[d * VW, 128], [128 * d * VW, cnt], [1, 128]])
                    P.dma("sp", vt[:, r * nb + nl:r * nb + nl + cnt, :], src, vs, writes=[vr])
            hk, hkr, hks = Hb.next()
            for j, z0 in enumerate((255, 127)):
                src = bass.AP(ed_t, gh * 384 + z0 - 127, [[1, 128], [1, 128]])
                P.dma("sp", hk[:, j, :], src, hks, writes=[hkr])
            fp_, fpr = ctx.psum[7]
            P.op("pe", lambda e, hk=hk, fp_=fp_: e.matmul(fp_[:, 0:256], lhsT=ctx.jflip[:, :], rhs=hk[:, :, :].rearrange("p j i -> p (j i)"), start=True, stop=True),
                 reads=[hkr, ctx.const_r], writes=[fpr])
            et, er = Eb.next()
            P.op("act", lambda e, et=et, fp_=fp_: e.activation(out=et[:, :, :].rearrange("p j i -> p (j i)"), in_=fp_[:, 0:256], func=AF.Copy), reads=[fpr], writes=[er])
            first = (g == 0)
            for r in range(d):
                for n0 in range(0, nb, 2):
                    nn = min(2, nb - n0)
                    sp_, spr = ctx.psum[1 + (n0 // 2) % 2]
                    items = []
                    for a in range(nn):
                        n = n0 + a
                        qcols = slice(r + d * 128 * n, r + d * 128 * n + d * 127 + 1, d)
                        for j in range(2):
                            kn = n - 1 + j
                            if kn < 0:
                                continue
                            kcols = slice(r + d * 128 * kn, r + d * 128 * kn + d * 127 + 1, d)
                            items.append((a, j, kn, qcols, kcols))
                    for ii, (a, j, kn, qcols, kcols) in enumerate(items):
                        P.op("pe", lambda e, sp_=sp_, a=a, j=j, qcols=qcols, kcols=kcols, kt=kt, qt=qt: e.matmul(sp_[:, (a * 2 + j) * 128:(a * 2 + j + 1) * 128], lhsT=kt[:, kcols], rhs=qt[:, qcols], start=True, stop=True),
                             reads=[kr_, qr], writes=[spr], signal=(ii == len(items) - 1))
                    lo = (items[0][0] * 2 + items[0][1]) * 128
                    hi = nn * 256
                    ebt, ebr = eb.next()
                    P.op("act", lambda e, ebt=ebt, sp_=sp_, lo=lo, hi=hi: e.activation(out=ebt[:, lo:hi], in_=sp_[:, lo:hi], func=AF.Exp), reads=[spr], writes=[ebr])
                    pt, ptr = pT.next()
                    for a in range(nn):
                        if a == 0 and lo != 0:
                            P.op("dve", lambda e, pt=pt, ebt=ebt, et=et: e.tensor_tensor(out=pt[:, 128:256], in0=ebt[:, 128:256], in1=et[:, 1, :], op=ALU.mult), reads=[ebr, er], writes=[ptr])
                        else:
                            P.op("dve", lambda e, pt=pt, ebt=ebt, et=et, a=a: e.tensor_tensor(out=pt[:, a * 256:(a + 1) * 256], in0=ebt[:, a * 256:(a + 1) * 256], in1=et[:, :, :].rearrange("p j i -> p (j i)"), op=ALU.mult), reads=[ebr, er], writes=[ptr])
                    op_, opr = ctx.psum[3 + (n0 // 2) % 2]
                    dp, dpr = ctx.psum[5 + (n0 // 2) % 2]
                    for a in range(nn):
                        its = [it for it in items if it[0] == a]
                        for ii, (a_, j, kn, qcols, kcols) in enumerate(its):
                            P.op("pe", lambda e, a=a, j=j, kn=kn, ii=ii, its=its, pt=pt, vt=vt, r=r, nb=nb, op_=op_: e.matmul(op_[:, a * 128:(a + 1) * 128], lhsT=vt[:, r * nb + kn, :], rhs=pt[:, (a * 2 + j) * 128:(a * 2 + j + 1) * 128], start=(ii == 0), stop=(ii == len(its) - 1)),
                                 reads=[vr, ptr], writes=[opr], signal=False)
                        for ii, (a_, j, kn, qcols, kcols) in enumerate(its):
                            P.op("pe", lambda e, a=a, j=j, ii=ii, its=its, pt=pt, dp=dp: e.matmul(dp[:, a * 128:(a + 1) * 128], lhsT=ctx.ones_b[:, :], rhs=pt[:, (a * 2 + j) * 128:(a * 2 + j + 1) * 128], start=(ii == 0), stop=(ii == len(its) - 1)),
                                 reads=[ptr, ctx.const_r], writes=[dpr, opr], signal=(a == nn - 1 and ii == len(its) - 1))
                    c0 = r + d * 128 * n0
                    cols = slice(c0, c0 + d * (128 * nn - 1) + 1, d)
                    if first:
                        P.op("dve", lambda e, cols=cols, op_=op_, nn=nn: e.tensor_copy(out=acc_o[:, cols], in_=op_[:, 0:nn * 128]), reads=[opr], writes=[acc_o_r])
                        P.op("act", lambda e, cols=cols, dp=dp, nn=nn: e.activation(out=acc_d[:, cols], in_=dp[:, 0:nn * 128], func=AF.Copy), reads=[dpr], writes=[acc_d_r])
                    else:
                        P.op("dve", lambda e, cols=cols, op_=op_, nn=nn: e.tensor_tensor(out=acc_o[:, cols], in0=acc_o[:, cols], in1=op_[:, 0:nn * 128], op=ALU.add), reads=[opr, acc_o_r], writes=[acc_o_r])
                        P.op("dve", lambda e, cols=cols, dp=dp, nn=nn: e.tensor_tensor(out=acc_d[:, cols], in0=acc_d[:, cols], in1=dp[:, 0:nn * 128], op=ALU.add), reads=[dpr, acc_d_r], writes=[acc_d_r])
        for c in range(S // 512):
            ts = slice(c * 512, (c + 1) * 512)
            P.op("dve", lambda e, ts=ts: e.reciprocal(out=acc_d[:, ts], in_=acc_d[:, ts]), reads=[acc_d_r], writes=[acc_d_r])
            ob, obr, obs = ctx.outb.next()
            P.op("dve", lambda e, ts=ts, ob=ob: e.tensor_tensor(out=ob[:, :], in0=acc_o[:, ts], in1=acc_d[:, ts], op=ALU.mult), reads=[acc_o_r, acc_d_r], writes=[obr])
            P.dma("sp", T["oT"][h * 128:(h + 1) * 128, ts], ob[:, :], obs, reads=[obr])


def input_shapes(cfg):
    D, S, FF, H, QL, KVL, DH = cfg["D"], cfg["S"], cfg["FF"], cfg["H"], cfg["QL"], cfg["KVL"], cfg["DH"]
    KT, FT = D // 128, FF // 128
    GW = 3 * DH * 128
    sh = {"xT": [D, S], "ffn_norm_l": [128, 4 * KT], "attn_norm_l": [128, 2 * KT], "kvsrc_l": [128, KT],
          "qlora_l": [128, QL // 128], "kvlora_l": [128, KVL // 128], "gq_l": [128, 2], "gk_l": [128, 2],
          "kns_l": [128, 3], "dqn_l": [128, 3],
          "wdq_l": [QL // 128, 128, D], "wdkv_l": [KVL // 128 + 1, 128, D], "wuq_l": [H, 128, (QL // 128) * 192],
          "wukv_l": [H, 128, (KVL // 128) * 256], "mla_wo_l": [KT, 128, H * 128],
          "wk_l": [3 * DH, 128, D], "wv": [D, GW], "dil_wq_l": [3 * DH, 128, D], "dil_wo_l": [KT, 128, DH * 128],
          "rel_bias": [cfg["NBUCK"], 3 * DH],
          "cos2": [64, S], "sinS": [64, S], "cmask": [4, 128, 512], "onehot": [3, cfg["NBUCK"], 384], "zmask": [DH, 384], "rm": [64, 64], "jflip": [128, 128]}
    for i in range(4):
        sh[f"wg_l{i}"] = [FT, 128, D]
        sh[f"wu_l{i}"] = [FT, 128, D]
        sh[f"wd_l{i}"] = [KT, 128, FF]
    return sh


def build_program(cfg, stages=None):
    nc = bass.Bass("TRN2", target_bir_lowering=False)
    D, S, FF, H, QL, KVL, DH = cfg["D"], cfg["S"], cfg["FF"], cfg["H"], cfg["QL"], cfg["KVL"], cfg["DH"]
    KT = D // 128
    GW = 3 * DH * 128
    T = {}
    for k, shp in input_shapes(cfg).items():
        T[k] = nc.dram_tensor(k, shp, F32, kind="ExternalInput").ap()
    out = nc.dram_tensor("out", [D, S], F32, kind="ExternalOutput").ap()
    hA = nc.dram_tensor("hA", [D, S], F32).ap()
    hB = nc.dram_tensor("hB", [D, S], F32).ap()
    T["qT"] = nc.dram_tensor("qT", [H, 192, S], BF16).ap()
    T["ckvT"] = nc.dram_tensor("ckvT", [KVL, S], BF16).ap()
    T["krT"] = nc.dram_tensor("krT", [64, S], BF16).ap()
    T["ssqr"] = nc.dram_tensor("ssqr", [1, S], F32).ap()
    T["oT"] = nc.dram_tensor("oT", [max(H, DH) * 128, S], BF16).ap()
    T["kshT"] = nc.dram_tensor("kshT", [3 * DH, 128, S], BF16).ap()
    T["vsh"] = nc.dram_tensor("vsh", [S, GW], BF16).ap()
    T["qdT"] = nc.dram_tensor("qdT", [3 * DH, 128, S], BF16).ap()
    T["Ed"] = nc.dram_tensor("Ed", [3 * DH, 384], F32).ap()

    ctx = Ctx()
    ctx.nc = nc
    ctx.P = P = Prog(nc)
    ctx.A = A = Arena(nc)
    ctx.psum_gen = 0
    ctx.const_r = Res()
    ctx.ones_f = A.alloc([128, 128], F32)
    ctx.ones_b = A.alloc([128, 128], BF16)
    ctx.eps_t = A.alloc([128, 1], F32)
    ctx.rm = A.alloc([64, 64], F32)
    ctx.jflip = A.alloc([128, 128], F32)
    P.op("pool", lambda e: e.memset(ctx.ones_f[:, :], 1.0), writes=[ctx.const_r])
    P.op("pool", lambda e: e.memset(ctx.ones_b[:, :], 1.0), writes=[ctx.const_r])
    P.op("pool", lambda e: e.memset(ctx.eps_t[:, :], RMS_EPS), writes=[ctx.const_r])
    cs = P.dsem()
    P.dma("sp", ctx.rm[:, :], T["rm"], cs, writes=[ctx.const_r])
    P.dma("sp", ctx.jflip[:, :], T["jflip"], cs, writes=[ctx.const_r])
    for nm, key, w in [("ffn_norm_sb", "ffn_norm_l", 4 * KT), ("attn_norm_sb", "attn_norm_l", 2 * KT), ("kvsrc_sb", "kvsrc_l", KT),
                       ("qlora_sb", "qlora_l", QL // 128), ("kvlora_sb", "kvlora_l", KVL // 128), ("gq_sb", "gq_l", 2), ("gk_sb", "gk_l", 2),
                       ("kns_sb", "kns_l", 3), ("dqn_sb", "dqn_l", 3)]:
        T[nm] = A.alloc([128, w], F32)
        P.dma("sp", T[nm][:, :], T[key], cs, writes=[ctx.const_r])
    P.op("dve", lambda e: e.tensor_scalar(out=T["dqn_sb"][:, :], in0=T["dqn_sb"][:, :], scalar1=128.0 ** -0.5, scalar2=None, op0=ALU.mult),
         reads=[ctx.const_r], writes=[ctx.const_r])
    ctx.arena_mark = A.off

    fn = T["ffn_norm_sb"]
    an = T["attn_norm_sb"]
    st = stages or ["ffn0", "mlaproj", "mlaattn", "mlawo", "ffn1", "kvsh", "ffn2", "dilq", "bias", "dilattn", "dilwo", "ffn3"]
    seq = [
        ("ffn0", lambda hi, ho: stage_ffn(ctx, hi, ho, fn[:, 0:KT], T["wg_l0"], T["wu_l0"], T["wd_l0"], D, FF, S), True),
        ("mlaproj", lambda hi, ho: stage_mla_proj(ctx, hi, an[:, 0:KT], T, cfg), False),
        ("mlaattn", lambda hi, ho: stage_mla_attn(ctx, T, cfg), False),
        ("mlawo", lambda hi, ho: stage_oproj(ctx, T["oT"][0:H * 128, :], T["mla_wo_l"], hi, ho, D, H, S), True),
        ("ffn1", lambda hi, ho: stage_ffn(ctx, hi, ho, fn[:, KT:2 * KT], T["wg_l1"], T["wu_l1"], T["wd_l1"], D, FF, S), True),
        ("kvsh", lambda hi, ho: stage_proj_headnorm(ctx, hi, T["kvsrc_sb"][:, :], T["wk_l"], 3 * DH, T["kns_sb"], lambda n: n // DH, T["kshT"], D, S,
                                                    v_w=T["wv"], v_out=T["vsh"], NV=GW // 512), False),
        ("ffn2", lambda hi, ho: stage_ffn(ctx, hi, ho, fn[:, 2 * KT:3 * KT], T["wg_l2"], T["wu_l2"], T["wd_l2"], D, FF, S), True),
        ("dilq", lambda hi, ho: stage_proj_headnorm(ctx, hi, an[:, KT:2 * KT], T["dil_wq_l"], 3 * DH, T["dqn_sb"], lambda n: n // DH, T["qdT"], D, S), False),
        ("bias", lambda hi, ho: stage_bias_table(ctx, T, cfg), False),
        ("dilattn", lambda hi, ho: stage_dil_attn(ctx, T, cfg), False),
        ("dilwo", lambda hi, ho: stage_oproj(ctx, T["oT"][0:DH * 128, :], T["dil_wo_l"], hi, ho, D, DH, S), True),
        ("ffn3", lambda hi, ho: stage_ffn(ctx, hi, ho, fn[:, 3 * KT:4 * KT], T["wg_l3"], T["wu_l3"], T["wd_l3"], D, FF, S), True),
    ]
    seq = [s for s in seq if s[0] in st]
    n_upd = sum(1 for s in seq if s[2])
    cur = T["xT"]
    pp = [hA, hB]
    ui = 0
    for name, f, upd in seq:
        if upd:
            ui += 1
            nxt = out if ui == n_upd else pp[ui % 2]
            f(cur, nxt)
            cur = nxt
        else:
            f(cur, None)
    if n_upd == 0:
        raise ValueError("no residual update stage")
    P.finish()
    ctx.stats = dict(nins=P.nins, nsem=P.nsem)
    return nc, ctx


def lay_w(W):
    K, N = W.shape
    return np.ascontiguousarray(W.reshape(K // 128, 128, N // 128, 128).transpose(2, 1, 0, 3).reshape(N // 128, 128, K))


def lay_w_heads(W, H, hw):
    K = W.shape[0]
    return np.ascontiguousarray(W.reshape(K // 128, 128, H, hw).transpose(2, 1, 0, 3).reshape(H, 128, (K // 128) * hw))


def lay_vec(v):
    return np.ascontiguousarray(v.reshape(-1, 128).T)


def t5_bucket(dist, nbuck, maxd):
    max_exact = nbuck // 2
    n = np.maximum(dist, 0)
    nf = np.maximum(n, 1).astype(np.float32)
    large = max_exact + (np.log(nf / np.float32(max_exact)) / np.float32(math.log(maxd / max_exact)) * np.float32(nbuck - max_exact)).astype(np.int32)
    large = np.minimum(large, nbuck - 1)
    return np.where(n < max_exact, n, large)


def const_tables(cfg):
    S, DH, NBK = cfg["S"], cfg["DH"], cfg["NBUCK"]
    half = 32
    inv = (np.float32(cfg["THETA"]) ** (-np.arange(half, dtype=np.float32) / np.float32(half))).astype(np.float32)
    ang = np.arange(S, dtype=np.float32)[:, None] * inv[None, :]
    cos = np.cos(ang).astype(np.float32).T
    sin = np.sin(ang).astype(np.float32).T
    cos2 = np.ascontiguousarray(np.concatenate([cos, cos], 0))
    sinS = np.ascontiguousarray(np.concatenate([-sin, sin], 0))
    c = np.arange(128)[:, None]
    q = np.arange(512)[None, :]
    cmask = np.stack([(128 * m + c <= q) for m in range(4)], 0).astype(np.float32)
    onehot = np.zeros((3, NBK, 384), np.float32)
    z = np.arange(384)
    delta = z - 127
    valid = (delta >= 0) & (delta <= 128)
    for g, (win, d) in enumerate(cfg["GROUPS"]):
        assert win // d == 128
        b = t5_bucket(np.clip(delta, 0, 128) * d, NBK, cfg["MAXD"])
        onehot[g, b[valid], z[valid]] = 1.0
    zmask = np.ascontiguousarray(np.broadcast_to(valid.astype(np.float32)[None, :], (DH, 384)))
    rm = np.zeros((64, 64), np.float32)
    for m in range(64):
        rm[(m + 32) % 64, m] = 1.0
    jflip = np.ascontiguousarray(np.eye(128, dtype=np.float32)[::-1])
    return dict(cos2=cos2, sinS=sinS, cmask=cmask, onehot=onehot, zmask=zmask, rm=rm, jflip=jflip)


def prepare_inputs(cfg, inp):
    D, S, FF, H, QL, KVL, DH = cfg["D"], cfg["S"], cfg["FF"], cfg["H"], cfg["QL"], cfg["KVL"], cfg["DH"]
    f = lambda a: np.asarray(a, dtype=np.float32)
    GW = 3 * DH * 128
    com = {}
    fnorm = f(inp["ffn_norm"])
    com["ffn_norm_l"] = np.ascontiguousarray(np.concatenate([lay_vec(fnorm[l, j]) for l in range(2) for j in range(2)], 1))
    an = f(inp["attn_norm"])
    com["attn_norm_l"] = np.ascontiguousarray(np.concatenate([lay_vec(an[0]), lay_vec(an[1])], 1))
    com["kvsrc_l"] = lay_vec(f(inp["kv_src_norm"]))
    com["qlora_l"] = lay_vec(f(inp["mla_q_lora_norm"])[0])
    com["kvlora_l"] = lay_vec(f(inp["mla_kv_lora_norm"])[0])

    def g2(v):
        o = np.zeros((128, 2), np.float32)
        o[:, 0] = v[0:128]
        o[0:64, 1] = v[128:192]
        return o
    com["gq_l"] = g2(f(inp["mla_q_norm"])[0])
    com["gk_l"] = g2(f(inp["mla_k_norm"])[0])
    com["kns_l"] = np.ascontiguousarray(f(inp["k_norm_shared"]).T)
    com["dqn_l"] = np.ascontiguousarray(f(inp["dil_q_norm"])[0].T)
    com["wdq_l"] = lay_w(f(inp["mla_wdq"])[0])
    wdkv = f(inp["mla_wdkv"])[0]
    wdkv_p = np.zeros((D, KVL + 128), np.float32)
    wdkv_p[:, :KVL + 64] = wdkv
    com["wdkv_l"] = lay_w(wdkv_p)
    com["wuq_l"] = lay_w_heads(f(inp["mla_wuq"])[0], H, 192)
    com["wukv_l"] = lay_w_heads(f(inp["mla_wukv"])[0], H, 256)
    com["mla_wo_l"] = lay_w(f(inp["mla_wo"])[0])
    wkv = f(inp["w_kv_shared"])
    com["wk_l"] = lay_w(wkv[:, :GW])
    com["wv"] = np.ascontiguousarray(wkv[:, GW:])
    com["dil_wq_l"] = lay_w(f(inp["dil_wq"])[0])
    com["dil_wo_l"] = lay_w(f(inp["dil_wo"])[0])
    com["rel_bias"] = f(inp["rel_bias"])
    wg, wu, wd = inp["ffn_wg"], inp["ffn_wu"], inp["ffn_wd"]
    i = 0
    for l in range(2):
        for j in range(2):
            com[f"wg_l{i}"] = lay_w(f(wg[l, j]))
            com[f"wu_l{i}"] = lay_w(f(wu[l, j]))
            com[f"wd_l{i}"] = lay_w(f(wd[l, j]))
            i += 1
    com.update(const_tables(cfg))
    x = f(inp["x"])
    return [dict(xT=np.ascontiguousarray(x[b].T), **com) for b in range(cfg["B"])]


_CACHE = {}


def kernel(**inputs):
    cfg = CFG_FULL
    if "nc" not in _CACHE:
        _CACHE["nc"] = build_program(cfg)[0]
    nc = _CACHE["nc"]
    in_maps = prepare_inputs(cfg, inputs)
    res = run_bass_kernel_spmd(nc, in_maps, core_ids=list(range(cfg["B"])))
    out = np.stack([np.ascontiguousarray(res.results[b]["out"].T) for b in range(cfg["B"])], 0)
    return out.astype(np.float32)
```

```python
import math
import os
import numpy as np
import concourse.bass as bass
import concourse.mybir as mybir
from concourse.bass_utils import run_bass_kernel_spmd

F32 = mybir.dt.float32
BF16 = mybir.dt.bfloat16
AF = mybir.ActivationFunctionType
ALU = mybir.AluOpType
RMS_EPS = 1e-6
SEM_ROT = 30000
DSEM_RETIRE = 20000

CFG_FULL = dict(D=4096, B=2, S=8192, FF=11008, H=32, QL=1024, KVL=512, DH=32,
                GROUPS=((128, 1), (512, 4), (2048, 16)), NBUCK=32, MAXD=2048, THETA=10000.0)


class Sem:
    def __init__(self, nc, name):
        self.h = nc.alloc_semaphore(name=name)
        self.v = 0


class Res:
    __slots__ = ("w", "r")

    def __init__(self):
        self.w = None
        self.r = {}


class Prog:
    ENG = ["sp", "act", "dve", "pool", "pe"]

    def __init__(self, nc):
        self.nc = nc
        self.q = {e: [] for e in self.ENG}
        self.nsem = 0
        self.pg = {e: self.new_sem("pg_" + e) for e in ["act", "dve", "pool", "pe"]}
        self.waited = {e: {} for e in self.ENG}
        self.nins = {e: 0 for e in self.ENG}
        self.pending = {e: [] for e in self.ENG}
        self.last_ev = {}
        self.dsems = []
        self.free_dsems = {'sp': [], 'pool': []}
        self.stage_dsems = []

    def new_sem(self, name):
        self.nsem += 1
        return Sem(self.nc, f"{name}_{self.nsem}")

    def dsem(self, kind="sp"):
        while self.free_dsems[kind] and self.free_dsems[kind][-1].v > DSEM_RETIRE:
            self.free_dsems[kind].pop()
        if self.free_dsems[kind]:
            s = self.free_dsems[kind].pop()
        else:
            s = self.new_sem("d" + kind)
            self.dsems.append(s)
        self.stage_dsems.append((kind, s))
        return s

    def release_stage_sems(self):
        for kind, s in self.stage_dsems:
            self.free_dsems[kind].append(s)
        self.stage_dsems = []

    def _wait(self, eng, sem, val):
        if eng == "pe" and sem is self.pg["pe"]:
            return
        w = self.waited[eng]
        if w.get(sem, 0) >= val:
            return
        w[sem] = val
        self.q[eng].append(lambda e, h=sem.h, v=val: e.wait_ge(h, v))
        self.nins[eng] += 1

    def _deps(self, eng, reads, writes):
        for b in reads:
            if b.w is not None:
                self._wait(eng, *b.w)
        for b in writes:
            if b.w is not None:
                self._wait(eng, *b.w)
            for sm, v in b.r.items():
                self._wait(eng, sm, v)

    def _commit(self, ev, reads, writes):
        sm, v = ev
        for b in reads:
            if b.r.get(sm, 0) < v:
                b.r[sm] = v
        for b in writes:
            b.w = ev
            b.r = {}

    def op(self, eng, fn, reads=(), writes=(), signal=True):
        self._deps(eng, reads, writes)
        self.nins[eng] += 1
        if not signal:
            self.q[eng].append(fn)
            self.pending[eng].append((reads, writes))
            return
        sem = self.pg[eng]
        if sem.v >= SEM_ROT:
            sem = self.pg[eng] = self.new_sem("pg_" + eng)
        sem.v += 1
        self.q[eng].append(lambda e, h=sem.h: fn(e).then_inc(h, 1))
        self.last_ev[eng] = (sem, sem.v)
        for r_, w_ in self.pending[eng]:
            self._commit((sem, sem.v), r_, w_)
        self.pending[eng] = []
        self._commit((sem, sem.v), reads, writes)

    def dma(self, eng, out, in_, sem, reads=(), writes=()):
        for b in writes:
            if b.w is not None and b.w[0] is sem and not b.r:
                b.w = None
        self._deps(eng, reads, writes)
        self.nins[eng] += 1
        sem.v += 16
        self.q[eng].append(lambda e, h=sem.h: e.dma_start(out=out, in_=in_).then_inc(h, 16))
        self._commit((sem, sem.v), reads, writes)

    def barrier(self):
        for e in self.ENG:
            assert not self.pending[e], e
        evs = list(self.last_ev.values())
        evs += [(s, s.v) for s in self.dsems if s.v > 0]
        for e in self.ENG:
            for sm, v in evs:
                self._wait(e, sm, v)

    def finish(self):
        self.barrier()
        nc = self.nc
        q = self.q
        with nc.Block() as block:
            @block.sync
            def _(e):
                for f in q["sp"]:
                    f(e)

            @block.scalar
            def _(e):
                for f in q["act"]:
                    f(e)

            @block.vector
            def _(e):
                for f in q["dve"]:
                    f(e)

            @block.gpsimd
            def _(e):
                for f in q["pool"]:
                    f(e)

            @block.tensor
            def _(e):
                for f in q["pe"]:
                    f(e)


SPLIT_TENSORS = True


class Arena:
    BASE = 17 * 1024
    TOP = 224 * 1024 - 2560

    def __init__(self, nc, words=None):
        self.nc = nc
        self.words = (self.TOP - self.BASE) // 4
        self.off = 0
        self.n = 0
        self.big = None if SPLIT_TENSORS else nc.alloc_sbuf_tensor("arena", [128, self.words], F32)

    def alloc(self, shape, dtype):
        n = int(np.prod(shape[1:]))
        w = (n + 1) // 2 if dtype == BF16 else n
        w = (w + 15) // 16 * 16
        assert self.off + w <= self.words, ("sbuf overflow", self.off, w)
        off = self.off
        self.off += w
        return self.view(off, shape, dtype)

    def view(self, off, shape, dtype):
        n = int(np.prod(shape[1:]))
        w = (n + 1) // 2 if dtype == BF16 else n
        assert off + w <= self.words
        if SPLIT_TENSORS:
            self.n += 1
            return self.nc.alloc_sbuf_tensor_at(f"sb{self.n}", [int(x) for x in shape], dtype, offset=self.BASE + off * 4)
        v = self.big[:, off:off + w]
        if dtype == BF16:
            v = v.bitcast(BF16)
        v = v[:, 0:n]
        if len(shape) == 3:
            v = v.rearrange("p (a b) -> p a b", b=shape[2])
        elif len(shape) == 4:
            v = v.rearrange("p (a b c) -> p a b c", b=shape[2], c=shape[3])
        if shape[0] < 128:
            v = v[0:shape[0]]
        return v


class Ring:
    REVIEW = 32

    def __init__(self, ctx, n, shape, dtype, dma=False):
        self.A = ctx.A
        self.shape, self.dtype = shape, dtype
        self.offs = []
        self.t = []
        for _ in range(n):
            self.offs.append(ctx.A.off)
            self.t.append(ctx.A.alloc(shape, dtype))
        self.r = [Res() for _ in range(n)]
        self.s = [ctx.P.dsem('pool' if dma == 'pool' else 'sp') for _ in range(n)] if dma else None
        self.i = 0
        self.n = n

    def next(self):
        k = self.i % self.n
        if SPLIT_TENSORS and self.i >= self.n and (self.i // self.n) % self.REVIEW == 0:
            self.t[k] = self.A.view(self.offs[k], self.shape, self.dtype)
        self.i += 1
        if self.s:
            return self.t[k], self.r[k], self.s[k]
        return self.t[k], self.r[k]


class Ctx:
    pass


def sl(ap, *idx):
    return ap[idx]


def rstd_from_ssq(ctx, ssq_ps, ssq_pr, out_t, out_r, n_feat, parts=128):
    P = ctx.P
    P.op("act", lambda e: e.activation(out=out_t, in_=ssq_ps, func=AF.Sqrt, scale=1.0 / n_feat, bias=ctx.eps_t[0:parts, 0:1]),
         reads=[ssq_pr, ctx.const_r], writes=[out_r])
    P.op("dve", lambda e: e.reciprocal(out=out_t, in_=out_t), reads=[out_r], writes=[out_r])


def norm_chunk(ctx, h_in, g_cols, c, xn, xn_r, xn_conf, hld, KT):
    P = ctx.P
    G = hld.t[0].shape[1]
    D = KT * 128
    ts = slice(c * 512, (c + 1) * 512)
    h_v = h_in.rearrange("(k p) t -> p k t", p=128)
    ssq_ps, ssq_pr = ctx.psum[0]
    for kg in range(KT // G):
        ht, hr, hs = hld.next()
        P.dma("sp", ht[:, :, :], h_v[:, kg * G:(kg + 1) * G, ts], hs, writes=[hr] + ctx.hld_conf.get(id(hr), []))
        for j in range(G):
            kt = kg * G + j
            sqt, sqr = ctx.sq.next()
            P.op("pool", lambda e, o=sqt, i=ht, j=j: e.tensor_tensor(out=o[:, :], in0=i[:, j, :], in1=i[:, j, :], op=ALU.mult),
                 reads=[hr], writes=[sqr])
            P.op("pe", lambda e, r=sqt, kt=kt: e.matmul(ssq_ps[:, :], lhsT=ctx.ones_f[:, :], rhs=r[:, :], start=(kt == 0), stop=(kt == KT - 1)),
                 reads=[sqr, ctx.const_r], writes=[ssq_pr])
    rstd_from_ssq(ctx, ssq_ps[:, :], ssq_pr, ctx.rstd[:, :], ctx.rstd_r, D)
    for kg in range(KT // G):
        ht, hr, hs = hld.next()
        P.dma("sp", ht[:, :, :], h_v[:, kg * G:(kg + 1) * G, ts], hs, writes=[hr] + ctx.hld_conf.get(id(hr), []))
        for j in range(G):
            kt = kg * G + j
            P.op("dve", lambda e, i=ht, j=j, kt=kt: e.scalar_tensor_tensor(out=xn[:, kt, :], in0=i[:, j, :], scalar=g_cols[:, kt:kt + 1], in1=ctx.rstd[:, :], op0=ALU.mult, op1=ALU.mult),
                 reads=[hr, ctx.rstd_r, ctx.const_r], writes=[xn_r[kt]] + xn_conf[kt])


def mm_acc(ctx, ps, ps_r, lhs_list, rhs_list, extra_reads):
    P = ctx.P
    n = len(lhs_list)
    for k in range(n):
        (la, lr), (ra, rr) = lhs_list[k], rhs_list[k]
        P.op("pe", lambda e, la=la, ra=ra, k=k: e.matmul(ps, lhsT=la, rhs=ra, start=(k == 0), stop=(k == n - 1)),
             reads=[lr, rr] + extra_reads, writes=[ps_r], signal=(k == n - 1))


def new_psum(ctx):
    if not SPLIT_TENSORS and getattr(ctx, "psum", None):
        return
    for g in reversed(getattr(ctx, "psum_guards", [])):
        g.__exit__(None, None, None)
    ctx.psum_guards = [ctx.nc.psum_tensor(f"ps{ctx.psum_gen}_{i}", [128, 512], F32) for i in range(8)]
    ctx.psum_gen += 1
    ctx.psum = [(g.__enter__(), Res()) for g in ctx.psum_guards]


def stage_begin(ctx):
    ctx.P.barrier()
    ctx.P.release_stage_sems()
    new_psum(ctx)
    ctx.A.off = ctx.arena_mark
    ctx.hld_conf = {}
    ctx.sq = Ring(ctx, 2, [128, 512], F32)
    ctx.rstd = ctx.A.alloc([128, 512], F32)
    ctx.rstd_r = Res()
    ctx.psi = 1


def stage_ffn(ctx, h_in, h_out, g_cols, wg_l, wu_l, wd_l, D, FF, S):
    P, A = ctx.P, ctx.A
    stage_begin(ctx)
    KT, FT, NCH = D // 128, FF // 128, S // 512
    G = min(4, KT)
    base = A.off
    xn = A.alloc([128, KT, 512], BF16)
    xn_r = [Res() for _ in range(KT)]
    hld = Ring(ctx, 2, [128, G, 512], F32, dma=True)
    WDW = FT * 64
    if A.off - base < 2 * WDW:
        A.off = base + 2 * WDW
    wd = [A.view(base + i * WDW, [128, FT, 128], BF16) for i in range(2)]
    wd_r = [Res() for _ in range(2)]
    wd_s = [P.dsem('pool') for _ in range(2)]

    def _ov(lo, hi):
        rs = [xn_r[kt] for kt in range(KT) if kt * 256 < hi and (kt + 1) * 256 > lo]
        for i in range(2):
            a = KT * 256 + i * G * 512
            if a < hi and a + G * 512 > lo:
                rs.append(hld.r[i])
        return rs
    wd_conf = [_ov(i * WDW, (i + 1) * WDW) for i in range(2)]
    xn_conf = [[wd_r[i] for i in range(2) if xn_r[kt] in wd_conf[i]] for kt in range(KT)]
    for j in range(2):
        ctx.hld_conf[id(hld.r[j])] = [wd_r[i] for i in range(2) if hld.r[j] in wd_conf[i]]
    act = A.alloc([128, FT, 512], BF16)
    act_r = [Res() for _ in range(FT)]
    wg = Ring(ctx, 2, [128, KT, 128], BF16, dma="pool")
    wu = Ring(ctx, 2, [128, KT, 128], BF16, dma="pool")
    sg = Ring(ctx, 2, [128, 512], F32)
    ot = Ring(ctx, 2, [128, 512], F32, dma=True)
    hr_ = Ring(ctx, 2, [128, 512], F32, dma=True)
    di = 0
    for c in range(NCH):
        ts = slice(c * 512, (c + 1) * 512)
        norm_chunk(ctx, h_in, g_cols, c, xn, xn_r, xn_conf, hld, KT)
        for f in range(FT):
            wgt, wgr, wgs = wg.next()
            wut, wur, wus = wu.next()
            P.dma("pool", wgt[:, :, :], wg_l[f].rearrange("p (k m) -> p k m", m=128), wgs, writes=[wgr])
            P.dma("pool", wut[:, :, :], wu_l[f].rearrange("p (k m) -> p k m", m=128), wus, writes=[wur])
            s = f % 2
            gp, gpr = ctx.psum[1 + 2 * s]
            up, upr = ctx.psum[2 + 2 * s]
            mm_acc(ctx, gp[:, :], gpr, [(wgt[:, kt, :], wgr) for kt in range(KT)], [(xn[:, kt, :], xn_r[kt]) for kt in range(KT)], [])
            mm_acc(ctx, up[:, :], upr, [(wut[:, kt, :], wur) for kt in range(KT)], [(xn[:, kt, :], xn_r[kt]) for kt in range(KT)], [])
            sgt, sgr = sg.next()
            P.op("act", lambda e, o=sgt, i=gp: e.activation(out=o[:, :], in_=i[:, :], func=AF.Silu), reads=[gpr], writes=[sgr])
            P.op("dve", lambda e, a=sgt, u=up, f=f: e.tensor_tensor(out=act[:, f, :], in0=u[:, :], in1=a[:, :], op=ALU.mult),
                 reads=[sgr, upr], writes=[act_r[f]])
        for n in range(KT):
            s = di % 2
            di += 1
            P.dma("pool", wd[s][:, :, :], wd_l[n].rearrange("p (k m) -> p k m", m=128), wd_s[s], writes=[wd_r[s]] + wd_conf[s])
            hrt, hrr, hrs = hr_.next()
            P.dma("sp", hrt[:, :], h_in[n * 128:(n + 1) * 128, ts], hrs, writes=[hrr])
            op_, opr = ctx.psum[5 + s]
            mm_acc(ctx, op_[:, :], opr, [(wd[s][:, ft, :], wd_r[s]) for ft in range(FT)], [(act[:, ft, :], act_r[ft]) for ft in range(FT)], [])
            ott, otr, ots = ot.next()
            P.op("dve", lambda e, o=ott, i=op_, h=hrt: e.scalar_tensor_tensor(out=o[:, :], in0=i[:, :], scalar=0.5, in1=h[:, :], op0=ALU.mult, op1=ALU.add),
                 reads=[opr, hrr], writes=[otr])
            P.dma("sp", h_out[n * 128:(n + 1) * 128, ts], ott[:, :], ots, reads=[otr])


def stage_oproj(ctx, oT, w_l, h_in, h_out, D, KI, S):
    P, A = ctx.P, ctx.A
    stage_begin(ctx)
    KT, NCH = D // 128, S // 512
    ob = Ring(ctx, 2, [128, KI, 512], BF16, dma=True)
    w = Ring(ctx, 2, [128, KI, 128], BF16, dma="pool")
    ot = Ring(ctx, 2, [128, 512], F32, dma=True)
    hr_ = Ring(ctx, 2, [128, 512], F32, dma=True)
    o_v = oT.rearrange("(k p) t -> p k t", p=128)
    for c in range(NCH):
        ts = slice(c * 512, (c + 1) * 512)
        obt, obr, obs = ob.next()
        P.dma("sp", obt[:, :, :], o_v[:, :, ts], obs, writes=[obr])
        for n in range(KT):
            wt, wr, ws = w.next()
            P.dma("pool", wt[:, :, :], w_l[n].rearrange("p (k m) -> p k m", m=128), ws, writes=[wr])
            hrt, hrr, hrs = hr_.next()
            P.dma("sp", hrt[:, :], h_in[n * 128:(n + 1) * 128, ts], hrs, writes=[hrr])
            op_, opr = ctx.psum[1 + n % 2]
            mm_acc(ctx, op_[:, :], opr, [(wt[:, k, :], wr) for k in range(KI)], [(obt[:, k, :], obr) for k in range(KI)], [])
            ott, otr, ots = ot.next()
            P.op("dve", lambda e, o=ott, i=op_, h=hrt: e.tensor_tensor(out=o[:, :], in0=i[:, :], in1=h[:, :], op=ALU.add),
                 reads=[opr, hrr], writes=[otr])
            P.dma("sp", h_out[n * 128:(n + 1) * 128, ts], ott[:, :], ots, reads=[otr])


def headnorm_epilogue(ctx, ps, psr, parts, g_col, n_feat, out_dram, ssq_extra=None, scale=None, tmp=None):
    P = ctx.P
    xs, xsr = ctx.xs.next()
    P.op("act", lambda e: e.activation(out=xs[0:parts, :], in_=ps, func=AF.Copy), reads=[psr], writes=[xsr])
    sqt, sqr = ctx.sq.next()
    P.op("pool", lambda e: e.tensor_tensor(out=sqt[0:parts, :], in0=xs[0:parts, :], in1=xs[0:parts, :], op=ALU.mult), reads=[xsr], writes=[sqr])
    ssq_ps, ssq_pr = ctx.psum[0]
    P.op("pe", lambda e: e.matmul(ssq_ps[:, :], lhsT=ctx.ones_f[0:parts, :], rhs=sqt[0:parts, :], start=True, stop=True),
         reads=[sqr, ctx.const_r], writes=[ssq_pr])
    rt, rr = ctx.rt.next()
    rstd_from_ssq(ctx, ssq_ps[0:parts, :], ssq_pr, rt[0:parts, :], rr, n_feat, parts)
    ob, obr, obs = ctx.outb.next()
    P.op("dve", lambda e: e.scalar_tensor_tensor(out=ob[0:parts, :], in0=xs[0:parts, :], scalar=g_col, in1=rt[0:parts, :], op0=ALU.mult, op1=ALU.mult),
         reads=[xsr, rr, ctx.const_r], writes=[obr])
    P.dma("sp", out_dram, ob[0:parts, :], obs, reads=[obr])


def stage_proj_headnorm(ctx, h_in, g_cols, w_l, NT, gn_sb, gmap, out_T, D, S, v_w=None, v_out=None, NV=0):
    P, A = ctx.P, ctx.A
    stage_begin(ctx)
    KT, NCH = D // 128, S // 512
    G = min(4, KT)
    xn = A.alloc([128, KT, 512], BF16)
    xn_r = [Res() for _ in range(KT)]
    xn_conf = [[] for _ in range(KT)]
    hld = Ring(ctx, 2, [128, G, 512], F32, dma=True)
    w = Ring(ctx, 2, [128, KT, 128], BF16, dma="pool")
    ctx.xs = Ring(ctx, 2, [128, 512], F32)
    ctx.rt = Ring(ctx, 2, [128, 512], F32)
    ctx.outb = Ring(ctx, 2, [128, 512], BF16, dma=True)
    if v_w is not None:
        wv = Ring(ctx, 2, [128, KT, 512], BF16, dma="pool")
        vb = Ring(ctx, 2, [128, 512], BF16, dma=True)
        v_wv = v_w.rearrange("(k p) n -> p k n", p=128)
    for c in range(NCH):
        ts = slice(c * 512, (c + 1) * 512)
        norm_chunk(ctx, h_in, g_cols, c, xn, xn_r, xn_conf, hld, KT)
        for n in range(NT):
            wt, wr, ws = w.next()
            P.dma("pool", wt[:, :, :], w_l[n].rearrange("p (k m) -> p k m", m=128), ws, writes=[wr])
            ps, psr = ctx.psum[1 + n % 2]
            mm_acc(ctx, ps[:, :], psr, [(wt[:, k, :], wr) for k in range(KT)], [(xn[:, k, :], xn_r[k]) for k in range(KT)], [])
            headnorm_epilogue(ctx, ps[:, :], psr, 128, gn_sb[:, gmap(n):gmap(n) + 1], 128, out_T[n, :, ts])
        if v_w is not None:
            for nv in range(NV):
                wt, wr, ws = wv.next()
                q4 = max(1, KT // 4)
                for a in range(0, KT, q4):
                    P.dma("pool", wt[:, a:a + q4, :], v_wv[:, a:a + q4, nv * 512:(nv + 1) * 512], ws, writes=[wr])
                for tb in range(4):
                    ps, psr = ctx.psum[3 + tb]
                    mm_acc(ctx, ps[:, :], psr, [(xn[:, k, tb * 128:(tb + 1) * 128], xn_r[k]) for k in range(KT)], [(wt[:, k, :], wr) for k in range(KT)], [])
                    vt, vr, vs = vb.next()
                    P.op("act", lambda e, o=vt, i=ps: e.activation(out=o[:, :], in_=i[:, :], func=AF.Copy), reads=[psr], writes=[vr])
                    P.dma("sp", v_out[c * 512 + tb * 128:c * 512 + (tb + 1) * 128, nv * 512:(nv + 1) * 512], vt[:, :], vs, reads=[vr])


def rope_apply(ctx, xr, xr_r, cs, cs_r, sn, sn_r, out_dram):
    P = ctx.P
    rp, rpr = ctx.psum[7]
    P.op("pe", lambda e: e.matmul(rp[0:64, :], lhsT=ctx.rm[0:64, :], rhs=xr[0:64, :], start=True, stop=True),
         reads=[xr_r, ctx.const_r], writes=[rpr])
    t1, t1r = ctx.xs.next()
    P.op("dve", lambda e: e.tensor_tensor(out=t1[0:64, :], in0=rp[0:64, :], in1=sn[0:64, :], op=ALU.mult), reads=[rpr, sn_r], writes=[t1r])
    t2, t2r = ctx.sq.next()
    P.op("pool", lambda e: e.tensor_tensor(out=t2[0:64, :], in0=xr[0:64, :], in1=cs[0:64, :], op=ALU.mult), reads=[xr_r, cs_r], writes=[t2r])
    ob, obr, obs = ctx.outb.next()
    P.op("dve", lambda e: e.tensor_tensor(out=ob[0:64, :], in0=t1[0:64, :], in1=t2[0:64, :], op=ALU.add), reads=[t1r, t2r], writes=[obr])
    P.dma("sp", out_dram, ob[0:64, :], obs, reads=[obr])


def stage_mla_proj(ctx, h_in, g_cols, T, cfg):
    P, A = ctx.P, ctx.A
    stage_begin(ctx)
    D, S, H, QL, KVL = cfg["D"], cfg["S"], cfg["H"], cfg["QL"], cfg["KVL"]
    KT, NCH, QT_, KVT = D // 128, S // 512, QL // 128, KVL // 128
    G = min(4, KT)
    xn = A.alloc([128, KT, 512], BF16)
    xn_r = [Res() for _ in range(KT)]
    xn_conf = [[] for _ in range(KT)]
    hld = Ring(ctx, 2, [128, G, 512], F32, dma=True)
    w = Ring(ctx, 2, [128, KT, 128], BF16, dma="pool")
    wq = Ring(ctx, 2, [128, QT_, 192], BF16, dma="pool")
    cq = A.alloc([128, QT_, 512], F32)
    cq_r = [Res() for _ in range(QT_)]
    cqn = A.alloc([128, QT_, 512], BF16)
    cqn_r = [Res() for _ in range(QT_)]
    ckv = A.alloc([128, KVT, 512], F32)
    ckv_r = [Res() for _ in range(KVT)]
    ctx.xs = Ring(ctx, 3, [128, 512], F32)
    ctx.rt = Ring(ctx, 2, [128, 512], F32)
    ctx.outb = Ring(ctx, 3, [128, 512], BF16, dma=True)
    cs = Ring(ctx, 2, [64, 512], F32, dma=True)
    sn = Ring(ctx, 2, [64, 512], F32, dma=True)
    xr = Ring(ctx, 4, [64, 512], F32)
    rq = A.alloc([128, 512], F32)
    rq_r = Res()
    srow = Ring(ctx, 2, [1, 512], F32, dma=True)
    xn_rhs = [(xn[:, k, :], xn_r[k]) for k in range(KT)]
    for c in range(NCH):
        ts = slice(c * 512, (c + 1) * 512)
        norm_chunk(ctx, h_in, g_cols, c, xn, xn_r, xn_conf, hld, KT)
        cst, csr, css = cs.next()
        snt, snr, sns = sn.next()
        P.dma("sp", cst[:, :], T["cos2"][:, ts], css, writes=[csr])
        P.dma("sp", snt[:, :], T["sinS"][:, ts], sns, writes=[snr])
        ssq2, ssq2r = ctx.psum[6]
        for n in range(QT_):
            wt, wr, ws = w.next()
            P.dma("pool", wt[:, :, :], T["wdq_l"][n].rearrange("p (k m) -> p k m", m=128), ws, writes=[wr])
            ps, psr = ctx.psum[1 + n % 2]
            mm_acc(ctx, ps[:, :], psr, [(wt[:, k, :], wr) for k in range(KT)], xn_rhs, [])
            P.op("act", lambda e, n=n, ps=ps: e.activation(out=cq[:, n, :], in_=ps[:, :], func=AF.Copy), reads=[psr], writes=[cq_r[n]])
            sqt, sqr = ctx.sq.next()
            P.op("pool", lambda e, n=n, o=sqt: e.tensor_tensor(out=o[:, :], in0=cq[:, n, :], in1=cq[:, n, :], op=ALU.mult), reads=[cq_r[n]], writes=[sqr])
            P.op("pe", lambda e, r=sqt, n=n: e.matmul(ssq2[:, :], lhsT=ctx.ones_f[:, :], rhs=r[:, :], start=(n == 0), stop=(n == QT_ - 1)),
                 reads=[sqr, ctx.const_r], writes=[ssq2r])
        rstd_from_ssq(ctx, ssq2[:, :], ssq2r, rq[:, :], rq_r, QL)
        for n in range(QT_):
            P.op("dve", lambda e, n=n: e.scalar_tensor_tensor(out=cqn[:, n, :], in0=cq[:, n, :], scalar=T["qlora_sb"][:, n:n + 1], in1=rq[:, :], op0=ALU.mult, op1=ALU.mult),
                 reads=[cq_r[n], rq_r, ctx.const_r], writes=[cqn_r[n]])
        for n in range(KVT + 1):
            wt, wr, ws = w.next()
            P.dma("pool", wt[:, :, :], T["wdkv_l"][n].rearrange("p (k m) -> p k m", m=128), ws, writes=[wr])
            ps, psr = ctx.psum[1 + n % 2]
            if n < KVT:
                mm_acc(ctx, ps[:, :], psr, [(wt[:, k, :], wr) for k in range(KT)], xn_rhs, [])
                P.op("act", lambda e, n=n, ps=ps: e.activation(out=ckv[:, n, :], in_=ps[:, :], func=AF.Copy), reads=[psr], writes=[ckv_r[n]])
                sqt, sqr = ctx.sq.next()
                P.op("pool", lambda e, n=n, o=sqt: e.tensor_tensor(out=o[:, :], in0=ckv[:, n, :], in1=ckv[:, n, :], op=ALU.mult), reads=[ckv_r[n]], writes=[sqr])
                P.op("pe", lambda e, r=sqt, n=n: e.matmul(ssq2[:, :], lhsT=ctx.ones_f[:, :], rhs=r[:, :], start=(n == 0), stop=(n == KVT - 1)),
                     reads=[sqr, ctx.const_r], writes=[ssq2r])
            else:
                mm_acc(ctx, ps[0:64, :], psr, [(wt[:, k, 0:64], wr) for k in range(KT)], xn_rhs, [])
                xrt, xrr = xr.next()
                P.op("act", lambda e, o=xrt, ps=ps: e.activation(out=o[0:64, :], in_=ps[0:64, :], func=AF.Copy), reads=[psr], writes=[xrr])
                sqt, sqr = ctx.sq.next()
                P.op("pool", lambda e, o=sqt, i=xrt: e.tensor_tensor(out=o[0:64, :], in0=i[0:64, :], in1=i[0:64, :], op=ALU.mult), reads=[xrr], writes=[sqr])
                rp, rpr = ctx.psum[7]
                P.op("pe", lambda e, r=sqt: e.matmul(rp[0:1, :], lhsT=ctx.ones_f[0:64, 0:1], rhs=r[0:64, :], start=True, stop=True),
                     reads=[sqr, ctx.const_r], writes=[rpr])
                srt, srr, srs = srow.next()
                P.op("act", lambda e, o=srt: e.activation(out=o[0:1, :], in_=rp[0:1, :], func=AF.Copy), reads=[rpr], writes=[srr])
                P.dma("sp", T["ssqr"][0:1, ts], srt[0:1, :], srs, reads=[srr])
                xg, xgr = xr.next()
                P.op("dve", lambda e, o=xg, i=xrt: e.tensor_scalar(out=o[0:64, :], in0=i[0:64, :], scalar1=T["gk_sb"][0:64, 1:2], scalar2=None, op0=ALU.mult),
                     reads=[xrr, ctx.const_r], writes=[xgr])
                rope_apply(ctx, xg, xgr, cst, csr, snt, snr, T["krT"][:, ts])
        rstd_from_ssq(ctx, ssq2[:, :], ssq2r, rq[:, :], rq_r, KVL)
        for n in range(KVT):
            ob, obr, obs = ctx.outb.next()
            P.op("dve", lambda e, n=n, ob=ob: e.scalar_tensor_tensor(out=ob[:, :], in0=ckv[:, n, :], scalar=T["kvlora_sb"][:, n:n + 1], in1=rq[:, :], op0=ALU.mult, op1=ALU.mult),
                 reads=[ckv_r[n], rq_r, ctx.const_r], writes=[obr])
            P.dma("sp", T["ckvT"][n * 128:(n + 1) * 128, ts], ob[:, :], obs, reads=[obr])
        cq_rhs = [(cqn[:, k, :], cqn_r[k]) for k in range(QT_)]
        for h in range(H):
            wt, wr, ws = wq.next()
            P.dma("pool", wt[:, :, :], T["wuq_l"][h].rearrange("p (k m) -> p k m", m=192), ws, writes=[wr])
            pn, pnr = ctx.psum[1 + h % 2]
            pr_, prr = ctx.psum[3 + h % 2]
            mm_acc(ctx, pn[:, :], pnr, [(wt[:, k, 0:128], wr) for k in range(QT_)], cq_rhs, [])
            mm_acc(ctx, pr_[0:64, :], prr, [(wt[:, k, 128:192], wr) for k in range(QT_)], cq_rhs, [])
            xs, xsr = ctx.xs.next()
            P.op("act", lambda e, xs=xs, pn=pn: e.activation(out=xs[:, :], in_=pn[:, :], func=AF.Copy), reads=[pnr], writes=[xsr])
            xrt, xrr = xr.next()
            P.op("act", lambda e, o=xrt, pr_=pr_: e.activation(out=o[0:64, :], in_=pr_[0:64, :], func=AF.Copy), reads=[prr], writes=[xrr])
            sq1, sq1r = ctx.sq.next()
            P.op("pool", lambda e, o=sq1, xs=xs: e.tensor_tensor(out=o[:, :], in0=xs[:, :], in1=xs[:, :], op=ALU.mult), reads=[xsr], writes=[sq1r])
            sq2, sq2r = ctx.sq.next()
            P.op("pool", lambda e, o=sq2, i=xrt: e.tensor_tensor(out=o[0:64, :], in0=i[0:64, :], in1=i[0:64, :], op=ALU.mult), reads=[xrr], writes=[sq2r])
            ssq_ps, ssq_pr = ctx.psum[0]
            P.op("pe", lambda e, r=sq1: e.matmul(ssq_ps[:, :], lhsT=ctx.ones_f[:, :], rhs=r[:, :], start=True, stop=False),
                 reads=[sq1r, ctx.const_r], writes=[ssq_pr], signal=False)
            P.op("pe", lambda e, r=sq2: e.matmul(ssq_ps[:, :], lhsT=ctx.ones_f[0:64, :], rhs=r[0:64, :], start=False, stop=True),
                 reads=[sq2r, ctx.const_r], writes=[ssq_pr])
            rt, rr = ctx.rt.next()
            rstd_from_ssq(ctx, ssq_ps[:, :], ssq_pr, rt[:, :], rr, 192)
            ob, obr, obs = ctx.outb.next()
            P.op("dve", lambda e, ob=ob, xs=xs, rt=rt: e.scalar_tensor_tensor(out=ob[:, :], in0=xs[:, :], scalar=T["gq_sb"][:, 0:1], in1=rt[:, :], op0=ALU.mult, op1=ALU.mult),
                 reads=[xsr, rr, ctx.const_r], writes=[obr])
            P.dma("sp", T["qT"][h, 0:128, ts], ob[:, :], obs, reads=[obr])
            xg, xgr = xr.next()
            P.op("dve", lambda e, o=xg, i=xrt, rt=rt: e.scalar_tensor_tensor(out=o[0:64, :], in0=i[0:64, :], scalar=T["gq_sb"][0:64, 1:2], in1=rt[0:64, :], op0=ALU.mult, op1=ALU.mult),
                 reads=[xrr, rr, ctx.const_r], writes=[xgr])
            rope_apply(ctx, xg, xgr, cst, csr, snt, snr, T["qT"][h, 128:192, ts])


def stage_mla_attn(ctx, T, cfg):
    P, A = ctx.P, ctx.A
    stage_begin(ctx)
    S, H, KVL = cfg["S"], cfg["H"], cfg["KVL"]
    NCH, KVT, NB = S // 512, KVL // 128, S // 128
    scale = 192.0 ** -0.5
    Kn = A.alloc([128, S], BF16); Kn_r = [Res() for _ in range(NCH)]
    Kr = A.alloc([64, S], BF16); Kr_r = [Res() for _ in range(NCH)]
    V = A.alloc([128, NB, 128], BF16); V_r = [Res() for _ in range(NCH)]
    Qn = A.alloc([128, S], BF16); Qn_r = Res(); Qn_s = P.dsem()
    Qr = A.alloc([64, S], BF16); Qr_r = Res(); Qr_s = P.dsem()
    msk = A.alloc([128, 4, 512], BF16); msk_r = Res(); msk_s = P.dsem('pool')
    P.dma("pool", msk[:, :, :], T["cmask"].rearrange("m p q -> p m q"), msk_s, writes=[msk_r])
    ck = Ring(ctx, 2, [128, KVT, 512], BF16, dma=True)
    krc = Ring(ctx, 2, [64, 512], BF16, dma=True)
    srow = Ring(ctx, 2, [1, 512], F32, dma=True)
    wkv = Ring(ctx, 2, [128, KVT, 256], BF16, dma="pool")
    ctx.xs = Ring(ctx, 2, [128, 512], F32)
    ctx.rt = Ring(ctx, 2, [128, 512], F32)
    pT = Ring(ctx, 3, [128, 512], BF16)
    rden = Ring(ctx, 2, [128, 512], F32)
    ctx.outb = Ring(ctx, 2, [128, 512], BF16, dma=True)
    ckv_v = T["ckvT"].rearrange("(k p) t -> p k t", p=128)
    for h in range(H):
        wt, wr, ws = wkv.next()
        P.dma("pool", wt[:, :, :], T["wukv_l"][h].rearrange("p (k m) -> p k m", m=256), ws, writes=[wr])
        P.dma("sp", Qn[:, :], T["qT"][h, 0:128, :], Qn_s, writes=[Qn_r])
        P.dma("sp", Qr[:, :], T["qT"][h, 128:192, :], Qr_s, writes=[Qr_r])
        for c in range(NCH):
            ts = slice(c * 512, (c + 1) * 512)
            ct, cr, cs_ = ck.next()
            P.dma("sp", ct[:, :, :], ckv_v[:, :, ts], cs_, writes=[cr])
            kt_, krr, krs = krc.next()
            P.dma("sp", kt_[:, :], T["krT"][:, ts], krs, writes=[krr])
            srt, srr, srs = srow.next()
            P.dma("sp", srt[0:1, :], T["ssqr"][0:1, ts], srs, writes=[srr])
            kp, kpr = ctx.psum[1 + c % 2]
            mm_acc(ctx, kp[:, :], kpr, [(wt[:, k, 0:128], wr) for k in range(KVT)], [(ct[:, k, :], cr) for k in range(KVT)], [])
            xs, xsr = ctx.xs.next()
            P.op("act", lambda e, xs=xs, kp=kp: e.activation(out=xs[:, :], in_=kp[:, :], func=AF.Copy), reads=[kpr], writes=[xsr])
            sqt, sqr = ctx.sq.next()
            P.op("pool", lambda e, o=sqt, xs=xs: e.tensor_tensor(out=o[:, :], in0=xs[:, :], in1=xs[:, :], op=ALU.mult), reads=[xsr], writes=[sqr])
            ssq_ps, ssq_pr = ctx.psum[0]
            P.op("pe", lambda e, r=sqt: e.matmul(ssq_ps[:, :], lhsT=ctx.ones_f[:, :], rhs=r[:, :], start=True, stop=False),
                 reads=[sqr, ctx.const_r], writes=[ssq_pr], signal=False)
            P.op("pe", lambda e, r=srt: e.matmul(ssq_ps[:, :], lhsT=ctx.ones_f[0:1, :], rhs=r[0:1, :], start=False, stop=True),
                 reads=[srr, ctx.const_r], writes=[ssq_pr])
            rt, rr = ctx.rt.next()
            rstd_from_ssq(ctx, ssq_ps[:, :], ssq_pr, rt[:, :], rr, 192)
            P.op("dve", lambda e, xs=xs, rt=rt, ts=ts: e.scalar_tensor_tensor(out=Kn[:, ts], in0=xs[:, :], scalar=T["gk_sb"][:, 0:1], in1=rt[:, :], op0=ALU.mult, op1=ALU.mult),
                 reads=[xsr, rr, ctx.const_r], writes=[Kn_r[c]])
            P.op("dve", lambda e, i=kt_, rt=rt, ts=ts: e.tensor_tensor(out=Kr[0:64, ts], in0=i[0:64, :], in1=rt[0:64, :], op=ALU.mult),
                 reads=[krr, rr], writes=[Kr_r[c]])
            vp, vpr = ctx.psum[3 + c % 2]
            for tb in range(4):
                mm_acc(ctx, vp[:, tb * 128:(tb + 1) * 128], vpr, [(ct[:, k, tb * 128:(tb + 1) * 128], cr) for k in range(KVT)], [(wt[:, k, 128:256], wr) for k in range(KVT)], [])
            P.op("act", lambda e, c=c, vp=vp: e.activation(out=V[:, c * 4:(c + 1) * 4, :], in_=vp[:, :].rearrange("p (a b) -> p a b", b=128), func=AF.Copy),
                 reads=[vpr], writes=[V_r[c]])
        for J in range(NCH):
            qs = slice(J * 512, (J + 1) * 512)
            op_, opr = ctx.psum[5] if J % 2 == 0 else ctx.psum[7]
            dp, dpr = ctx.psum[6] if J % 2 == 0 else ctx.psum[4]
            nkb = 4 * J + 4
            def emit_qk(kb, qs=qs):
                kc = kb // 4
                ks = slice(kb * 128, (kb + 1) * 128)
                sp_, spr = ctx.psum[1 + kb % 2]
                P.op("pe", lambda e, sp_=sp_, ks=ks, qs=qs: e.matmul(sp_[:, :], lhsT=Kn[:, ks], rhs=Qn[:, qs], start=True, stop=False),
                     reads=[Kn_r[kc], Qn_r], writes=[spr], signal=False)
                P.op("pe", lambda e, sp_=sp_, ks=ks, qs=qs: e.matmul(sp_[:, :], lhsT=Kr[0:64, ks], rhs=Qr[0:64, qs], start=False, stop=True),
                     reads=[Kr_r[kc], Qr_r], writes=[spr])
                return sp_, spr

            def emit_rest(kb, sp_, spr, J=J, nkb=nkb, op_=op_, opr=opr, dp=dp, dpr=dpr):
                kc = kb // 4
                pt, ptr = pT.next()
                P.op("act", lambda e, pt=pt, sp_=sp_: e.activation(out=pt[:, :], in_=sp_[:, :], func=AF.Exp, scale=scale), reads=[spr], writes=[ptr])
                m = kb - 4 * J
                if m >= 0:
                    P.op("dve", lambda e, pt=pt, m=m: e.tensor_tensor(out=pt[:, :], in0=pt[:, :], in1=msk[:, m, :], op=ALU.mult), reads=[ptr, msk_r], writes=[ptr])
                P.op("pe", lambda e, pt=pt, kb=kb, op_=op_, nkb=nkb: e.matmul(op_[:, :], lhsT=V[:, kb, :], rhs=pt[:, :], start=(kb == 0), stop=(kb == nkb - 1)),
                     reads=[V_r[kc], ptr], writes=[opr], signal=False)
                P.op("pe", lambda e, pt=pt, kb=kb, dp=dp, nkb=nkb: e.matmul(dp[:, :], lhsT=ctx.ones_b[:, :], rhs=pt[:, :], start=(kb == 0), stop=(kb == nkb - 1)),
                     reads=[ptr, ctx.const_r], writes=[dpr, opr])

            prev = None
            for kb in range(nkb):
                cur = (kb,) + emit_qk(kb)
                if prev is not None:
                    emit_rest(*prev)
                prev = cur
            emit_rest(*prev)
            rd, rdr = rden.next()
            P.op("dve", lambda e, rd=rd, dp=dp: e.reciprocal(out=rd[:, :], in_=dp[:, :]), reads=[dpr], writes=[rdr])
            ob, obr, obs = ctx.outb.next()
            P.op("dve", lambda e, ob=ob, rd=rd, op_=op_: e.tensor_tensor(out=ob[:, :], in0=op_[:, :], in1=rd[:, :], op=ALU.mult), reads=[opr, rdr], writes=[obr])
            P.dma("sp", T["oT"][h * 128:(h + 1) * 128, qs], ob[:, :], obs, reads=[obr])


def stage_bias_table(ctx, T, cfg):
    P, A = ctx.P, ctx.A
    stage_begin(ctx)
    DH, NBK = cfg["DH"], cfg["NBUCK"]
    rb = A.alloc([NBK, 3 * DH], F32); rb_r = Res(); rb_s = P.dsem()
    P.dma("sp", rb[:, :], T["rel_bias"], rb_s, writes=[rb_r])
    oh = A.alloc([NBK, 3, 384], F32); oh_r = Res(); oh_s = P.dsem()
    P.dma("sp", oh[:, :, :], T["onehot"].rearrange("g b z -> b g z"), oh_s, writes=[oh_r])
    zm = A.alloc([DH, 384], F32); zm_r = Res(); zm_s = P.dsem()
    P.dma("sp", zm[:, :], T["zmask"], zm_s, writes=[zm_r])
    ed = Ring(ctx, 2, [DH, 384], F32, dma=True)
    for g in range(3):
        ps, psr = ctx.psum[1 + g % 2]
        P.op("pe", lambda e, g=g, ps=ps: e.matmul(ps[0:DH, 0:384], lhsT=rb[:, g * DH:(g + 1) * DH], rhs=oh[:, g, :], start=True, stop=True),
             reads=[rb_r, oh_r], writes=[psr])
        et, er, es = ed.next()
        P.op("act", lambda e, et=et, ps=ps: e.activation(out=et[:, :], in_=ps[0:DH, 0:384], func=AF.Exp), reads=[psr], writes=[er])
        P.op("dve", lambda e, et=et: e.tensor_tensor(out=et[:, :], in0=et[:, :], in1=zm[:, :], op=ALU.mult), reads=[er, zm_r], writes=[er])
        P.dma("sp", T["Ed"][g * DH:(g + 1) * DH, :], et[:, :], es, reads=[er])


def stage_dil_attn(ctx, T, cfg):
    P, A = ctx.P, ctx.A
    stage_begin(ctx)
    S, DH, GROUPS = cfg["S"], cfg["DH"], cfg["GROUPS"]
    NBLK = S // 128
    VW = 3 * DH * 128
    acc_o = A.alloc([128, S], F32); acc_o_r = Res()
    acc_d = A.alloc([128, S], F32); acc_d_r = Res()
    Qb = Ring(ctx, 2, [128, S], BF16, dma=True)
    Kb = Ring(ctx, 2, [128, S], BF16, dma=True)
    Vb = Ring(ctx, 2, [128, NBLK, 128], BF16, dma=True)
    Eb = Ring(ctx, 2, [128, 2, 128], F32)
    Hb = Ring(ctx, 2, [128, 2, 128], F32, dma=True)
    eb = Ring(ctx, 2, [128, 512], F32)
    pT = Ring(ctx, 2, [128, 512], BF16)
    ctx.outb = Ring(ctx, 2, [128, 512], BF16, dma=True)
    ed_t = T["Ed"].tensor
    for h in range(DH):
        for g, (win, d) in enumerate(GROUPS):
            gh = g * DH + h
            nb = NBLK // d
            qt, qr, qs_ = Qb.next()
            kt, kr_, ks_ = Kb.next()
            vt, vr, vs = Vb.next()
            P.dma("sp", qt[:, :], T["qdT"][gh, :, :], qs_, writes=[qr])
            P.dma("sp", kt[:, :], T["kshT"][gh, :, :], ks_, writes=[kr_])
            for r in range(d):
                for nl in range(0, nb, 16):
                    cnt = min(16, nb - nl)
                    src = bass.AP(T["vsh"].tensor, (r + d * 128 * nl) * VW + gh * 128, [[d * VW, 128], [128 * d * VW, cnt], [1, 128]])
                    P.dma("sp", vt[:, r * nb + nl:r * nb + nl + cnt, :], src, vs, writes=[vr])
            hk, hkr, hks = Hb.next()
            for j, z0 in enumerate((255, 127)):
                src = bass.AP(ed_t, gh * 384 + z0 - 127, [[1, 128], [1, 128]])
                P.dma("sp", hk[:, j, :], src, hks, writes=[hkr])
            fp_, fpr = ctx.psum[7]
            P.op("pe", lambda e, hk=hk, fp_=fp_: e.matmul(fp_[:, 0:256], lhsT=ctx.jflip[:, :], rhs=hk[:, :, :].rearrange("p j i -> p (j i)"), start=True, stop=True),
                 reads=[hkr, ctx.const_r], writes=[fpr])
            et, er = Eb.next()
            P.op("act", lambda e, et=et, fp_=fp_: e.activation(out=et[:, :, :].rearrange("p j i -> p (j i)"), in_=fp_[:, 0:256], func=AF.Copy), reads=[fpr], writes=[er])
            first = (g == 0)
            for r in range(d):
                for n0 in range(0, nb, 2):
                    nn = min(2, nb - n0)
                    sp_, spr = ctx.psum[1 + (n0 // 2) % 2]
                    items = []
                    for a in range(nn):
                        n = n0 + a
                        qcols = slice(r + d * 128 * n, r + d * 128 * n + d * 127 + 1, d)
                        for j in range(2):
                            kn = n - 1 + j
                            if kn < 0:
                                continue
                            kcols = slice(r + d * 128 * kn, r + d * 128 * kn + d * 127 + 1, d)
                            items.append((a, j, kn, qcols, kcols))
                    for ii, (a, j, kn, qcols, kcols) in enumerate(items):
                        P.op("pe", lambda e, sp_=sp_, a=a, j=j, qcols=qcols, kcols=kcols, kt=kt, qt=qt: e.matmul(sp_[:, (a * 2 + j) * 128:(a * 2 + j + 1) * 128], lhsT=kt[:, kcols], rhs=qt[:, qcols], start=True, stop=True),
                             reads=[kr_, qr], writes=[spr], signal=(ii == len(items) - 1))
                    lo = (items[0][0] * 2 + items[0][1]) * 128
                    hi = nn * 256
                    ebt, ebr = eb.next()
                    P.op("act", lambda e, ebt=ebt, sp_=sp_, lo=lo, hi=hi: e.activation(out=ebt[:, lo:hi], in_=sp_[:, lo:hi], func=AF.Exp), reads=[spr], writes=[ebr])
                    pt, ptr = pT.next()
                    for a in range(nn):
                        if a == 0 and lo != 0:
                            P.op("dve", lambda e, pt=pt, ebt=ebt, et=et: e.tensor_tensor(out=pt[:, 128:256], in0=ebt[:, 128:256], in1=et[:, 1, :], op=ALU.mult), reads=[ebr, er], writes=[ptr])
                        else:
                            P.op("dve", lambda e, pt=pt, ebt=ebt, et=et, a=a: e.tensor_tensor(out=pt[:, a * 256:(a + 1) * 256], in0=ebt[:, a * 256:(a + 1) * 256], in1=et[:, :, :].rearrange("p j i -> p (j i)"), op=ALU.mult), reads=[ebr, er], writes=[ptr])
                    op_, opr = ctx.psum[3 + (n0 // 2) % 2]
                    dp, dpr = ctx.psum[5 + (n0 // 2) % 2]
                    for a in range(nn):
                        its = [it for it in items if it[0] == a]
                        for ii, (a_, j, kn, qcols, kcols) in enumerate(its):
                            P.op("pe", lambda e, a=a, j=j, kn=kn, ii=ii, its=its, pt=pt, vt=vt, r=r, nb=nb, op_=op_: e.matmul(op_[:, a * 128:(a + 1) * 128], lhsT=vt[:, r * nb + kn, :], rhs=pt[:, (a * 2 + j) * 128:(a * 2 + j + 1) * 128], start=(ii == 0), stop=(ii == len(its) - 1)),
                                 reads=[vr, ptr], writes=[opr], signal=False)
                        for ii, (a_, j, kn, qcols, kcols) in enumerate(its):
                            P.op("pe", lambda e, a=a, j=j, ii=ii, its=its, pt=pt, dp=dp: e.matmul(dp[:, a * 128:(a + 1) * 128], lhsT=ctx.ones_b[:, :], rhs=pt[:, (a * 2 + j) * 128:(a * 2 + j + 1) * 128], start=(ii == 0), stop=(ii == len(its) - 1)),
                                 reads=[ptr, ctx.const_r], writes=[dpr, opr], signal=(a == nn - 1 and ii == len(its) - 1))
                    c0 = r + d * 128 * n0
                    cols = slice(c0, c0 + d * (128 * nn - 1) + 1, d)
                    if first:
                        P.op("dve", lambda e, cols=cols, op_=op_, nn=nn: e.tensor_copy(out=acc_o[:, cols], in_=op_[:, 0:nn * 128]), reads=[opr], writes=[acc_o_r])
                        P.op("act", lambda e, cols=cols, dp=dp, nn=nn: e.activation(out=acc_d[:, cols], in_=dp[:, 0:nn * 128], func=AF.Copy), reads=[dpr], writes=[acc_d_r])
                    else:
                        P.op("dve", lambda e, cols=cols, op_=op_, nn=nn: e.tensor_tensor(out=acc_o[:, cols], in0=acc_o[:, cols], in1=op_[:, 0:nn * 128], op=ALU.add), reads=[opr, acc_o_r], writes=[acc_o_r])
                        P.op("dve", lambda e, cols=cols, dp=dp, nn=nn: e.tensor_tensor(out=acc_d[:, cols], in0=acc_d[:, cols], in1=dp[:, 0:nn * 128], op=ALU.add), reads=[dpr, acc_d_r], writes=[acc_d_r])
        for c in range(S // 512):
            ts = slice(c * 512, (c + 1) * 512)
            P.op("dve", lambda e, ts=ts: e.reciprocal(out=acc_d[:, ts], in_=acc_d[:, ts]), reads=[acc_d_r], writes=[acc_d_r])
            ob, obr, obs = ctx.outb.next()
            P.op("dve", lambda e, ts=ts, ob=ob: e.tensor_tensor(out=ob[:, :], in0=acc_o[:, ts], in1=acc_d[:, ts], op=ALU.mult), reads=[acc_o_r, acc_d_r], writes=[obr])
            P.dma("sp", T["oT"][h * 128:(h + 1) * 128, ts], ob[:, :], obs, reads=[obr])


def input_shapes(cfg):
    D, S, FF, H, QL, KVL, DH = cfg["D"], cfg["S"], cfg["FF"], cfg["H"], cfg["QL"], cfg["KVL"], cfg["DH"]
    KT, FT = D // 128, FF // 128
    GW = 3 * DH * 128
    sh = {"xT": [D, S], "ffn_norm_l": [128, 4 * KT], "attn_norm_l": [128, 2 * KT], "kvsrc_l": [128, KT],
          "qlora_l": [128, QL // 128], "kvlora_l": [128, KVL // 128], "gq_l": [128, 2], "gk_l": [128, 2],
          "kns_l": [128, 3], "dqn_l": [128, 3],
          "wdq_l": [QL // 128, 128, D], "wdkv_l": [KVL // 128 + 1, 128, D], "wuq_l": [H, 128, (QL // 128) * 192],
          "wukv_l": [H, 128, (KVL // 128) * 256], "mla_wo_l": [KT, 128, H * 128],
          "wk_l": [3 * DH, 128, D], "wv": [D, GW], "dil_wq_l": [3 * DH, 128, D], "dil_wo_l": [KT, 128, DH * 128],
          "rel_bias": [cfg["NBUCK"], 3 * DH],
          "cos2": [64, S], "sinS": [64, S], "cmask": [4, 128, 512], "onehot": [3, cfg["NBUCK"], 384], "zmask": [DH, 384], "rm": [64, 64], "jflip": [128, 128]}
    for i in range(4):
        sh[f"wg_l{i}"] = [FT, 128, D]
        sh[f"wu_l{i}"] = [FT, 128, D]
        sh[f"wd_l{i}"] = [KT, 128, FF]
    return sh


def build_program(cfg, stages=None):
    nc = bass.Bass("TRN2", target_bir_lowering=False)
    D, S, FF, H, QL, KVL, DH = cfg["D"], cfg["S"], cfg["FF"], cfg["H"], cfg["QL"], cfg["KVL"], cfg["DH"]
    KT = D // 128
    GW = 3 * DH * 128
    T = {}
    for k, shp in input_shapes(cfg).items():
        T[k] = nc.dram_tensor(k, shp, F32, kind="ExternalInput").ap()
    out = nc.dram_tensor("out", [D, S], F32, kind="ExternalOutput").ap()
    hA = nc.dram_tensor("hA", [D, S], F32).ap()
    hB = nc.dram_tensor("hB", [D, S], F32).ap()
    T["qT"] = nc.dram_tensor("qT", [H, 192, S], BF16).ap()
    T["ckvT"] = nc.dram_tensor("ckvT", [KVL, S], BF16).ap()
    T["krT"] = nc.dram_tensor("krT", [64, S], BF16).ap()
    T["ssqr"] = nc.dram_tensor("ssqr", [1, S], F32).ap()
    T["oT"] = nc.dram_tensor("oT", [max(H, DH) * 128, S], BF16).ap()
    T["kshT"] = nc.dram_tensor("kshT", [3 * DH, 128, S], BF16).ap()
    T["vsh"] = nc.dram_tensor("vsh", [S, GW], BF16).ap()
    T["qdT"] = nc.dram_tensor("qdT", [3 * DH, 128, S], BF16).ap()
    T["Ed"] = nc.dram_tensor("Ed", [3 * DH, 384], F32).ap()

    ctx = Ctx()
    ctx.nc = nc
    ctx.P = P = Prog(nc)
    ctx.A = A = Arena(nc)
    ctx.psum_gen = 0
    ctx.const_r = Res()
    ctx.ones_f = A.alloc([128, 128], F32)
    ctx.ones_b = A.alloc([128, 128], BF16)
    ctx.eps_t = A.alloc([128, 1], F32)
    ctx.rm = A.alloc([64, 64], F32)
    ctx.jflip = A.alloc([128, 128], F32)
    P.op("pool", lambda e: e.memset(ctx.ones_f[:, :], 1.0), writes=[ctx.const_r])
    P.op("pool", lambda e: e.memset(ctx.ones_b[:, :], 1.0), writes=[ctx.const_r])
    P.op("pool", lambda e: e.memset(ctx.eps_t[:, :], RMS_EPS), writes=[ctx.const_r])
    cs = P.dsem()
    P.dma("sp", ctx.rm[:, :], T["rm"], cs, writes=[ctx.const_r])
    P.dma("sp", ctx.jflip[:, :], T["jflip"], cs, writes=[ctx.const_r])
    for nm, key, w in [("ffn_norm_sb", "ffn_norm_l", 4 * KT), ("attn_norm_sb", "attn_norm_l", 2 * KT), ("kvsrc_sb", "kvsrc_l", KT),
                       ("qlora_sb", "qlora_l", QL // 128), ("kvlora_sb", "kvlora_l", KVL // 128), ("gq_sb", "gq_l", 2), ("gk_sb", "gk_l", 2),
                       ("kns_sb", "kns_l", 3), ("dqn_sb", "dqn_l", 3)]:
        T[nm] = A.alloc([128, w], F32)
        P.dma("sp", T[nm][:, :], T[key], cs, writes=[ctx.const_r])
    P.op("dve", lambda e: e.tensor_scalar(out=T["dqn_sb"][:, :], in0=T["dqn_sb"][:, :], scalar1=128.0 ** -0.5, scalar2=None, op0=ALU.mult),
         reads=[ctx.const_r], writes=[ctx.const_r])
    ctx.arena_mark = A.off

    fn = T["ffn_norm_sb"]
    an = T["attn_norm_sb"]
    st = stages or ["ffn0", "mlaproj", "mlaattn", "mlawo", "ffn1", "kvsh", "ffn2", "dilq", "bias", "dilattn", "dilwo", "ffn3"]
    seq = [
        ("ffn0", lambda hi, ho: stage_ffn(ctx, hi, ho, fn[:, 0:KT], T["wg_l0"], T["wu_l0"], T["wd_l0"], D, FF, S), True),
        ("mlaproj", lambda hi, ho: stage_mla_proj(ctx, hi, an[:, 0:KT], T, cfg), False),
        ("mlaattn", lambda hi, ho: stage_mla_attn(ctx, T, cfg), False),
        ("mlawo", lambda hi, ho: stage_oproj(ctx, T["oT"][0:H * 128, :], T["mla_wo_l"], hi, ho, D, H, S), True),
        ("ffn1", lambda hi, ho: stage_ffn(ctx, hi, ho, fn[:, KT:2 * KT], T["wg_l1"], T["wu_l1"], T["wd_l1"], D, FF, S), True),
        ("kvsh", lambda hi, ho: stage_proj_headnorm(ctx, hi, T["kvsrc_sb"][:, :], T["wk_l"], 3 * DH, T["kns_sb"], lambda n: n // DH, T["kshT"], D, S,
                                                    v_w=T["wv"], v_out=T["vsh"], NV=GW // 512), False),
        ("ffn2", lambda hi, ho: stage_ffn(ctx, hi, ho, fn[:, 2 * KT:3 * KT], T["wg_l2"], T["wu_l2"], T["wd_l2"], D, FF, S), True),
        ("dilq", lambda hi, ho: stage_proj_headnorm(ctx, hi, an[:, KT:2 * KT], T["dil_wq_l"], 3 * DH, T["dqn_sb"], lambda n: n // DH, T["qdT"], D, S), False),
        ("bias", lambda hi, ho: stage_bias_table(ctx, T, cfg), False),
        ("dilattn", lambda hi, ho: stage_dil_attn(ctx, T, cfg), False),
        ("dilwo", lambda hi, ho: stage_oproj(ctx, T["oT"][0:DH * 128, :], T["dil_wo_l"], hi, ho, D, DH, S), True),
        ("ffn3", lambda hi, ho: stage_ffn(ctx, hi, ho, fn[:, 3 * KT:4 * KT], T["wg_l3"], T["wu_l3"], T["wd_l3"], D, FF, S), True),
    ]
    seq = [s for s in seq if s[0] in st]
    n_upd = sum(1 for s in seq if s[2])
    cur = T["xT"]
    pp = [hA, hB]
    ui = 0
    for name, f, upd in seq:
        if upd:
            ui += 1
            nxt = out if ui == n_upd else pp[ui % 2]
            f(cur, nxt)
            cur = nxt
        else:
            f(cur, None)
    if n_upd == 0:
        raise ValueError("no residual update stage")
    P.finish()
    ctx.stats = dict(nins=P.nins, nsem=P.nsem)
    return nc, ctx


def lay_w(W):
    K, N = W.shape
    return np.ascontiguousarray(W.reshape(K // 128, 128, N // 128, 128).transpose(2, 1, 0, 3).reshape(N // 128, 128, K))


def lay_w_heads(W, H, hw):
    K = W.shape[0]
    return np.ascontiguousarray(W.reshape(K // 128, 128, H, hw).transpose(2, 1, 0, 3).reshape(H, 128, (K // 128) * hw))


def lay_vec(v):
    return np.ascontiguousarray(v.reshape(-1, 128).T)


def t5_bucket(dist, nbuck, maxd):
    max_exact = nbuck // 2
    n = np.maximum(dist, 0)
    nf = np.maximum(n, 1).astype(np.float32)
    large = max_exact + (np.log(nf / np.float32(max_exact)) / np.float32(math.log(maxd / max_exact)) * np.float32(nbuck - max_exact)).astype(np.int32)
    large = np.minimum(large, nbuck - 1)
    return np.where(n < max_exact, n, large)


def const_tables(cfg):
    S, DH, NBK = cfg["S"], cfg["DH"], cfg["NBUCK"]
    half = 32
    inv = (np.float32(cfg["THETA"]) ** (-np.arange(half, dtype=np.float32) / np.float32(half))).astype(np.float32)
    ang = np.arange(S, dtype=np.float32)[:, None] * inv[None, :]
    cos = np.cos(ang).astype(np.float32).T
    sin = np.sin(ang).astype(np.float32).T
    cos2 = np.ascontiguousarray(np.concatenate([cos, cos], 0))
    sinS = np.ascontiguousarray(np.concatenate([-sin, sin], 0))
    c = np.arange(128)[:, None]
    q = np.arange(512)[None, :]
    cmask = np.stack([(128 * m + c <= q) for m in range(4)], 0).astype(np.float32)
    onehot = np.zeros((3, NBK, 384), np.float32)
    z = np.arange(384)
    delta = z - 127
    valid = (delta >= 0) & (delta <= 128)
    for g, (win, d) in enumerate(cfg["GROUPS"]):
        assert win // d == 128
        b = t5_bucket(np.clip(delta, 0, 128) * d, NBK, cfg["MAXD"])
        onehot[g, b[valid], z[valid]] = 1.0
    zmask = np.ascontiguousarray(np.broadcast_to(valid.astype(np.float32)[None, :], (DH, 384)))
    rm = np.zeros((64, 64), np.float32)
    for m in range(64):
        rm[(m + 32) % 64, m] = 1.0
    jflip = np.ascontiguousarray(np.eye(128, dtype=np.float32)[::-1])
    return dict(cos2=cos2, sinS=sinS, cmask=cmask, onehot=onehot, zmask=zmask, rm=rm, jflip=jflip)


def prepare_inputs(cfg, inp):
    D, S, FF, H, QL, KVL, DH = cfg["D"], cfg["S"], cfg["FF"], cfg["H"], cfg["QL"], cfg["KVL"], cfg["DH"]
    f = lambda a: np.asarray(a, dtype=np.float32)
    GW = 3 * DH * 128
    com = {}
    fnorm = f(inp["ffn_norm"])
    com["ffn_norm_l"] = np.ascontiguousarray(np.concatenate([lay_vec(fnorm[l, j]) for l in range(2) for j in range(2)], 1))
    an = f(inp["attn_norm"])
    com["attn_norm_l"] = np.ascontiguousarray(np.concatenate([lay_vec(an[0]), lay_vec(an[1])], 1))
    com["kvsrc_l"] = lay_vec(f(inp["kv_src_norm"]))
    com["qlora_l"] = lay_vec(f(inp["mla_q_lora_norm"])[0])
    com["kvlora_l"] = lay_vec(f(inp["mla_kv_lora_norm"])[0])

    def g2(v):
        o = np.zeros((128, 2), np.float32)
        o[:, 0] = v[0:128]
        o[0:64, 1] = v[128:192]
        return o
    com["gq_l"] = g2(f(inp["mla_q_norm"])[0])
    com["gk_l"] = g2(f(inp["mla_k_norm"])[0])
    com["kns_l"] = np.ascontiguousarray(f(inp["k_norm_shared"]).T)
    com["dqn_l"] = np.ascontiguousarray(f(inp["dil_q_norm"])[0].T)
    com["wdq_l"] = lay_w(f(inp["mla_wdq"])[0])
    wdkv = f(inp["mla_wdkv"])[0]
    wdkv_p = np.zeros((D, KVL + 128), np.float32)
    wdkv_p[:, :KVL + 64] = wdkv
    com["wdkv_l"] = lay_w(wdkv_p)
    com["wuq_l"] = lay_w_heads(f(inp["mla_wuq"])[0], H, 192)
    com["wukv_l"] = lay_w_heads(f(inp["mla_wukv"])[0], H, 256)
    com["mla_wo_l"] = lay_w(f(inp["mla_wo"])[0])
    wkv = f(inp["w_kv_shared"])
    com["wk_l"] = lay_w(wkv[:, :GW])
    com["wv"] = np.ascontiguousarray(wkv[:, GW:])
    com["dil_wq_l"] = lay_w(f(inp["dil_wq"])[0])
    com["dil_wo_l"] = lay_w(f(inp["dil_wo"])[0])
    com["rel_bias"] = f(inp["rel_bias"])
    wg, wu, wd = inp["ffn_wg"], inp["ffn_wu"], inp["ffn_wd"]
    i = 0
    for l in range(2):
        for j in range(2):
            com[f"wg_l{i}"] = lay_w(f(wg[l, j]))
            com[f"wu_l{i}"] = lay_w(f(wu[l, j]))
            com[f"wd_l{i}"] = lay_w(f(wd[l, j]))
            i += 1
    com.update(const_tables(cfg))
    x = f(inp["x"])
    return [dict(xT=np.ascontiguousarray(x[b].T), **com) for b in range(cfg["B"])]


_CACHE = {}


def kernel(**inputs):
    cfg = CFG_FULL
    if "nc" not in _CACHE:
        _CACHE["nc"] = build_program(cfg)[0]
    nc = _CACHE["nc"]
    in_maps = prepare_inputs(cfg, inputs)
    res = run_bass_kernel_spmd(nc, in_maps, core_ids=list(range(cfg["B"])))
    out = np.stack([np.ascontiguousarray(res.results[b]["out"].T) for b in range(cfg["B"])], 0)
    return out.astype(np.float32)
```
